# Optimizing a Trainium2 kernel written in Bass

```python
import math
import jax, jax.numpy as jnp
from jax import lax
import numpy as np

D_MODEL = 1024
BATCH = 8
SEQ = 2048
DEPTH = 4

GRID_W = 64
CTX_LEN = 256
EPS = 1e-6
SHORT_CONV = 3

HY_WIDTH = 512
HY_ORDER = 2
HY_BANDS = 16
HY_EMB = 1 + 2 * HY_BANDS
HY_HIDDEN = 64
HY_SHORT_DECAY_PCT = 0.3
HY_LONG_DECAY_PCT = 1.5
HY_TARGET = 1e-2

SSM_HEADS = 8
SSM_HEAD_DIM = 64
SSM_WIDTH = SSM_HEADS * SSM_HEAD_DIM
SSM_GROUPS = 2
SSM_HPG = SSM_HEADS // SSM_GROUPS
SSM_STATE = 128
SSM_CHUNK = 128

GDN_HEADS = 4
GDN_DK = 128
GDN_DV = 128
GDN_CHUNK = 64

N_BRANCH = 3
D_FF = -(-8 * D_MODEL // (3 * 256)) * 256

HY_COLS = 3 * HY_WIDTH
SSM_XBC = SSM_WIDTH + 2 * SSM_GROUPS * SSM_STATE
SSM_COLS = SSM_WIDTH + SSM_XBC + 2 * SSM_HEADS
GDN_QKV = GDN_HEADS * (2 * GDN_DK + GDN_DV)
GDN_COLS = GDN_QKV + GDN_HEADS * GDN_DV + 4 * GDN_HEADS
GATE_COLS = N_BRANCH * D_MODEL
OFF_SSM = HY_COLS
OFF_GDN = OFF_SSM + SSM_COLS
OFF_GATE = OFF_GDN + GDN_COLS
IN_COLS = OFF_GATE + GATE_COLS

kernel_name = 'hybrid_hyena_ssd_gdn_prefix_dit'


def rmsnorm(x, w):
    xf = x.astype(jnp.float32)
    y = xf * lax.rsqrt(jnp.mean(xf * xf, axis=-1, keepdims=True) + EPS)
    return (y * w.astype(jnp.float32)).astype(x.dtype)


def l2norm(x):
    return x * lax.rsqrt(jnp.sum(x * x, axis=-1, keepdims=True) + EPS)


def rev(a):
    return a[:, ::-1]


def short_conv(u, w, n_rows):
    bsz, L, ch = u.shape
    row = L // n_rows
    pad = SHORT_CONV // 2
    ur = jnp.pad(u.reshape(bsz, n_rows, row, ch), ((0, 0), (0, 0), (pad, pad), (0, 0)))
    y = sum(ur[:, :, j:j + row] * w[j] for j in range(SHORT_CONV))
    return y.reshape(bsz, L, ch)


def hyena_filters(L, w1, b1, w2, b2, w3, freq):
    t = jnp.linspace(0.0, 1.0, L, dtype=jnp.float32)[:, None]
    w = 2.0 * math.pi * jnp.arange(L, dtype=jnp.float32)[:, None] / L
    f = jnp.linspace(1e-4, HY_BANDS - 1, HY_BANDS, dtype=jnp.float32)[None, :]
    z = jnp.concatenate([t, jnp.cos(f * w), -jnp.sin(f * w)], axis=-1)
    h = jnp.sin(freq[0] * (z @ w1 + b1))
    h = jnp.sin(freq[1] * (h @ w2 + b2))
    h = (h @ w3).astype(jnp.float32).reshape(L, HY_ORDER, 2, HY_WIDTH)
    min_decay = math.log(HY_TARGET) / HY_LONG_DECAY_PCT
    max_decay = math.log(HY_TARGET) / HY_SHORT_DECAY_PCT
    deltas = jnp.linspace(min_decay, max_decay, HY_WIDTH, dtype=jnp.float32)
    window = jnp.exp(-t * jnp.abs(deltas))
    return h * window[:, None, None, :]


def bidir_long_conv(u, h, bias):
    L = u.shape[1]
    k = jnp.concatenate([h[:, 0], jnp.zeros_like(h[:1, 0]), h[:0:-1, 1]], axis=0)
    spec = jnp.fft.rfft(u, n=2 * L, axis=1) * jnp.fft.rfft(k, axis=0)[None]
    y = jnp.fft.irfft(spec, n=2 * L, axis=1)[:, :L]
    return y + u * bias


def hyena_mixer(p, n_rows, conv_w, conv_b, w1, b1, w2, b2, w3, freq, bias):
    L = p.shape[1]
    u = (short_conv(p, conv_w, n_rows) + conv_b).astype(jnp.float32)
    v, x1, x2 = jnp.split(u, 3, axis=-1)
    h = hyena_filters(L, w1, b1, w2, b2, w3, freq)
    z = x1 * bidir_long_conv(v, h[:, 0], bias[0])
    z = x2 * bidir_long_conv(z, h[:, 1], bias[1])
    return z.astype(p.dtype)


def ssd_chunked(x, dt, A, Bm, Cm, h0):
    bsz, L = x.shape[:2]
    nc, Q = L // SSM_CHUNK, SSM_CHUNK

    def chunk(a):
        return a.reshape(bsz, nc, Q, *a.shape[2:])

    a_cum = jnp.cumsum(chunk(dt * A), axis=2)
    xd = chunk(x * dt[..., None])
    Bc, Cc = chunk(Bm), chunk(Cm)
    causal = jnp.tril(jnp.ones((Q, Q), dtype=bool))[:, :, None, None]
    decay_ls = jnp.exp(jnp.where(causal, a_cum[:, :, :, None] - a_cum[:, :, None, :], -jnp.inf))
    cb = jnp.einsum('bclgn,bcsgn->bclsg', Cc, Bc)
    y_diag = jnp.einsum('bclsg,bclsge,bcsgep->bclgep', cb, decay_ls, xd)
    a_last = a_cum[:, :, -1]
    states = jnp.einsum('bcsgn,bcsge,bcsgep->bcgepn', Bc, jnp.exp(a_last[:, :, None] - a_cum), xd)

    def step(h, inp):
        st, dec = inp
        return h * dec[..., None, None] + st, h

    h_last, h_in = lax.scan(step, h0, (jnp.moveaxis(states, 1, 0), jnp.moveaxis(jnp.exp(a_last), 1, 0)))
    h_in = jnp.moveaxis(h_in, 0, 1)
    y_off = jnp.einsum('bclgn,bcgepn,bclge->bclgep', Cc, h_in, jnp.exp(a_cum))
    return (y_diag + y_off).reshape(x.shape), h_last


def ssm_prepare(p, n_rows, conv_w, conv_b, dt_bias):
    bsz, L, _ = p.shape
    z = p[..., :SSM_WIDTH]
    xbc = jax.nn.silu(short_conv(p[..., SSM_WIDTH:SSM_WIDTH + SSM_XBC], conv_w, n_rows) + conv_b).astype(jnp.float32)
    nb = SSM_GROUPS * SSM_STATE
    xs = xbc[..., :SSM_WIDTH].reshape(bsz, L, SSM_GROUPS, SSM_HPG, SSM_HEAD_DIM)
    Bm = xbc[..., SSM_WIDTH:SSM_WIDTH + nb].reshape(bsz, L, SSM_GROUPS, SSM_STATE)
    Cm = xbc[..., SSM_WIDTH + nb:].reshape(bsz, L, SSM_GROUPS, SSM_STATE)
    dt_raw = p[..., SSM_WIDTH + SSM_XBC:].astype(jnp.float32).reshape(bsz, L, 2, SSM_GROUPS, SSM_HPG)
    dt = jax.nn.softplus(dt_raw + dt_bias.astype(jnp.float32).reshape(2, SSM_GROUPS, SSM_HPG))
    return z, xs, Bm, Cm, dt


def ssm_bidir(xs, Bm, Cm, dt, A, h0_f, h0_b):
    y_f, h_f = ssd_chunked(xs, dt[:, :, 0], A[0], Bm, Cm, h0_f)
    y_b, h_b = ssd_chunked(rev(xs), rev(dt[:, :, 1]), A[1], rev(Bm), rev(Cm), h0_b)
    return y_f + rev(y_b), h_f, h_b


def ssm_output(y, xs, z, D_skip, norm_w):
    bsz, L = y.shape[:2]
    y = y + xs * D_skip.astype(jnp.float32).reshape(SSM_GROUPS, SSM_HPG)[..., None]
    gw = SSM_HPG * SSM_HEAD_DIM
    y = y.reshape(bsz, L, SSM_GROUPS, gw) * jax.nn.silu(z.astype(jnp.float32)).reshape(bsz, L, SSM_GROUPS, gw)
    y = rmsnorm(y, norm_w.reshape(SSM_GROUPS, gw))
    return y.reshape(bsz, L, SSM_WIDTH)


def ssm_mixer(p_ctx, p_lat, n_rows, conv_w, conv_b, dt_bias, A_log, D_skip, norm_w):
    A = -jnp.exp(A_log.astype(jnp.float32)).reshape(2, SSM_GROUPS, SSM_HPG)
    zc, xc, Bc, Cc, dtc = ssm_prepare(p_ctx, 1, conv_w, conv_b, dt_bias)
    zl, xl, Bl, Cl, dtl = ssm_prepare(p_lat, n_rows, conv_w, conv_b, dt_bias)
    h0 = jnp.zeros((p_ctx.shape[0], SSM_GROUPS, SSM_HPG, SSM_HEAD_DIM, SSM_STATE), jnp.float32)
    yc, h_f, h_b = ssm_bidir(xc, Bc, Cc, dtc, A, h0, h0)
    yl, _, _ = ssm_bidir(xl, Bl, Cl, dtl, A, h_f, h_b)
    return (ssm_output(yc, xc, zc, D_skip, norm_w).astype(p_ctx.dtype),
            ssm_output(yl, xl, zl, D_skip, norm_w).astype(p_lat.dtype))


def gdn_chunked(q, k, v, g, beta, S0):
    bsz, L, H, _ = q.shape
    nc, C = L // GDN_CHUNK, GDN_CHUNK

    def chunk(a):
        return jnp.moveaxis(a.reshape(bsz, nc, C, H, *a.shape[3:]), 3, 2)

    q, k, v, g, beta = chunk(q), chunk(k), chunk(v), chunk(g), chunk(beta)
    g_cum = jnp.cumsum(g, axis=-1)
    incl = jnp.tril(jnp.ones((C, C), dtype=bool))
    decay = jnp.exp(jnp.where(incl, g_cum[..., :, None] - g_cum[..., None, :], -jnp.inf))
    kb = k * beta[..., None]
    mat = jnp.einsum('bnhid,bnhjd->bnhij', kb, k) * decay
    rhs = jnp.concatenate([v * beta[..., None], kb * jnp.exp(g_cum)[..., None]], axis=-1)
    sol = lax.linalg.triangular_solve(mat, rhs, left_side=True, lower=True, unit_diagonal=True)
    u, w = sol[..., :GDN_DV], sol[..., GDN_DV:]
    qk = jnp.einsum('bnhid,bnhjd->bnhij', q, k) * decay
    qg = q * jnp.exp(g_cum)[..., None]
    g_last = g_cum[..., -1]
    kd = k * jnp.exp(g_last[..., None] - g_cum)[..., None]

    def step(S, inp):
        u_c, w_c, qk_c, qg_c, kd_c, gl_c = inp
        v_new = u_c - jnp.einsum('bhck,bhkv->bhcv', w_c, S)
        o = jnp.einsum('bhck,bhkv->bhcv', qg_c, S) + jnp.einsum('bhij,bhjv->bhiv', qk_c, v_new)
        S = S * jnp.exp(gl_c)[..., None, None] + jnp.einsum('bhck,bhcv->bhkv', kd_c, v_new)
        return S, o

    xs = (jnp.moveaxis(u, 1, 0), jnp.moveaxis(w, 1, 0), jnp.moveaxis(qk, 1, 0),
          jnp.moveaxis(qg, 1, 0), jnp.moveaxis(kd, 1, 0), jnp.moveaxis(g_last, 1, 0))
    S_last, o = lax.scan(step, S0, xs)
    o = jnp.moveaxis(jnp.moveaxis(o, 0, 1), 2, 3).reshape(bsz, L, H, GDN_DV)
    return o, S_last


def gdn_prepare(p, n_rows, conv_w, dt_bias, A_log):
    bsz, L, _ = p.shape
    nq = GDN_HEADS * GDN_DK
    nv = GDN_HEADS * GDN_DV
    qkv = jax.nn.silu(short_conv(p[..., :GDN_QKV], conv_w, n_rows)).astype(jnp.float32)
    q = l2norm(qkv[..., :nq].reshape(bsz, L, GDN_HEADS, GDN_DK)) * GDN_DK ** -0.5
    k = l2norm(qkv[..., nq:2 * nq].reshape(bsz, L, GDN_HEADS, GDN_DK))
    v = qkv[..., 2 * nq:].reshape(bsz, L, GDN_HEADS, GDN_DV)
    gate = p[..., GDN_QKV:GDN_QKV + nv]
    o0 = GDN_QKV + nv
    a = p[..., o0:o0 + 2 * GDN_HEADS].astype(jnp.float32).reshape(bsz, L, 2, GDN_HEADS)
    b = p[..., o0 + 2 * GDN_HEADS:].astype(jnp.float32).reshape(bsz, L, 2, GDN_HEADS)
    g = -jnp.exp(A_log.astype(jnp.float32)) * jax.nn.softplus(a + dt_bias.astype(jnp.float32))
    beta = jax.nn.sigmoid(b)
    return q, k, v, g, beta, gate


def gdn_bidir(q, k, v, g, beta, S_f, S_b):
    o_f, S_f = gdn_chunked(q, k, v, g[:, :, 0], beta[:, :, 0], S_f)
    o_b, S_b = gdn_chunked(rev(q), rev(k), rev(v), rev(g[:, :, 1]), rev(beta[:, :, 1]), S_b)
    return o_f + rev(o_b), S_f, S_b


def gdn_output(o, gate, norm_w):
    bsz, L = o.shape[:2]
    o = rmsnorm(o, norm_w) * jax.nn.silu(gate.astype(jnp.float32)).reshape(bsz, L, GDN_HEADS, GDN_DV)
    return o.reshape(bsz, L, GDN_HEADS * GDN_DV)


def gdn_mixer(p_ctx, p_lat, n_rows, conv_w, dt_bias, A_log, norm_w):
    qc, kc, vc, gc, bc, gtc = gdn_prepare(p_ctx, 1, conv_w, dt_bias, A_log)
    ql, kl, vl, gl, bl, gtl = gdn_prepare(p_lat, n_rows, conv_w, dt_bias, A_log)
    S0 = jnp.zeros((p_ctx.shape[0], GDN_HEADS, GDN_DK, GDN_DV), jnp.float32)
    oc, S_f, S_b = gdn_bidir(qc, kc, vc, gc, bc, S0, S0)
    ol, _, _ = gdn_bidir(ql, kl, vl, gl, bl, S_f, S_b)
    return (gdn_output(oc, gtc, norm_w).astype(p_ctx.dtype),
            gdn_output(ol, gtl, norm_w).astype(p_lat.dtype))


def branch_merge(p_gate, y_hy, y_ssm, y_gdn, w_hy_out, w_ssm_out, w_gdn_out, w_out):
    gates = jax.nn.sigmoid(p_gate.reshape(*p_gate.shape[:-1], N_BRANCH, D_MODEL))
    m = (gates[..., 0, :] * (y_hy @ w_hy_out)
         + gates[..., 1, :] * (y_ssm @ w_ssm_out)
         + gates[..., 2, :] * (y_gdn @ w_gdn_out))
    return m @ w_out


def swiglu(h, w_gate_up, w_down):
    gu = h @ w_gate_up
    return (jax.nn.silu(gu[..., :D_FF]) * gu[..., D_FF:]) @ w_down


def setup_inputs(seed: int = 0) -> dict:
    key = jax.random.key(seed)
    ks = iter(jax.random.split(key, 40))
    D = D_MODEL
    L_ = DEPTH

    def nrm(shape, s):
        return jax.random.normal(next(ks), shape, jnp.float32) * s

    def gain(shape):
        return 1.0 + nrm(shape, 0.02)

    def dt_bias(shape):
        dt = jnp.exp(jax.random.uniform(next(ks), shape, jnp.float32, math.log(1e-3), math.log(1e-1)))
        return dt + jnp.log(-jnp.expm1(-dt))

    def a_log(shape):
        return jnp.log(jax.random.uniform(next(ks), shape, jnp.float32, 1.0, 16.0))

    return {
        'x': nrm((BATCH, SEQ, D), 1.0),
        'c': nrm((BATCH, D), 1.0),
        'ctx': nrm((BATCH, CTX_LEN, D), 1.0),
        'c_ctx': nrm((D,), 1.0),
        'w_ada': nrm((L_, D, 6 * D), 0.5 * D ** -0.5),
        'b_ada': nrm((L_, 6 * D), 0.02),
        'norm1_w': gain((L_, D)),
        'norm2_w': gain((L_, D)),
        'w_in': nrm((L_, D, IN_COLS), D ** -0.5),
        'hy_conv_w': nrm((L_, SHORT_CONV, HY_COLS), SHORT_CONV ** -0.5),
        'hy_conv_b': nrm((L_, HY_COLS), 0.02),
        'hy_w1': nrm((L_, HY_EMB, HY_HIDDEN), HY_EMB ** -0.5),
        'hy_b1': nrm((L_, HY_HIDDEN), HY_EMB ** -0.5),
        'hy_w2': nrm((L_, HY_HIDDEN, HY_HIDDEN), HY_HIDDEN ** -0.5),
        'hy_b2': nrm((L_, HY_HIDDEN), HY_HIDDEN ** -0.5),
        'hy_w3': nrm((L_, HY_HIDDEN, HY_ORDER * 2 * HY_WIDTH), 0.04 * HY_HIDDEN ** -0.5),
        'hy_freq': gain((L_, 2, HY_HIDDEN)),
        'hy_bias': nrm((L_, HY_ORDER, HY_WIDTH), 0.5),
        'ssm_conv_w': nrm((L_, SHORT_CONV, SSM_XBC), SHORT_CONV ** -0.5),
        'ssm_conv_b': nrm((L_, SSM_XBC), 0.02),
        'ssm_dt_bias': dt_bias((L_, 2, SSM_HEADS)),
        'ssm_A_log': a_log((L_, 2, SSM_HEADS)),
        'ssm_D': gain((L_, SSM_HEADS)),
        'ssm_norm_w': gain((L_, SSM_WIDTH)),
        'gdn_conv_w': nrm((L_, SHORT_CONV, GDN_QKV), SHORT_CONV ** -0.5),
        'gdn_dt_bias': dt_bias((L_, 2, GDN_HEADS)),
        'gdn_A_log': a_log((L_, 2, GDN_HEADS)),
        'gdn_norm_w': gain((L_, GDN_DV)),
        'w_hy_out': nrm((L_, HY_WIDTH, D), HY_WIDTH ** -0.5),
        'w_ssm_out': nrm((L_, SSM_WIDTH, D), SSM_WIDTH ** -0.5),
        'w_gdn_out': nrm((L_, GDN_HEADS * GDN_DV, D), (GDN_HEADS * GDN_DV) ** -0.5),
        'w_out': nrm((L_, D, D), D ** -0.5),
        'w_gate_up': nrm((L_, D, 2 * D_FF), D ** -0.5),
        'w_down': nrm((L_, D_FF, D), D_FF ** -0.5),
        'final_norm_w': gain((D,)),
    }


def reference(x, c, ctx, c_ctx, w_ada, b_ada, norm1_w, norm2_w, w_in,
              hy_conv_w, hy_conv_b, hy_w1, hy_b1, hy_w2, hy_b2, hy_w3, hy_freq, hy_bias,
              ssm_conv_w, ssm_conv_b, ssm_dt_bias, ssm_A_log, ssm_D, ssm_norm_w,
              gdn_conv_w, gdn_dt_bias, gdn_A_log, gdn_norm_w,
              w_hy_out, w_ssm_out, w_gdn_out, w_out, w_gate_up, w_down, final_norm_w):
    bsz, n_lat, _ = x.shape
    rows = n_lat // GRID_W
    xl, xc = x, ctx
    s_lat = jax.nn.silu(c)
    s_ctx = jax.nn.silu(c_ctx)
    for l in range(DEPTH):
        last = l == DEPTH - 1
        mod_l = (s_lat @ w_ada[l] + b_ada[l]).reshape(bsz, 6, D_MODEL)[:, :, None, :]
        mod_c = (s_ctx @ w_ada[l] + b_ada[l]).reshape(6, D_MODEL)

        hl = rmsnorm(xl, norm1_w[l]) * (1 + mod_l[:, 1]) + mod_l[:, 0]
        hc = rmsnorm(xc, norm1_w[l]) * (1 + mod_c[1]) + mod_c[0]
        pl = hl @ w_in[l]
        pc = hc @ w_in[l]
        ssm_c, ssm_l = ssm_mixer(pc[..., OFF_SSM:OFF_GDN], pl[..., OFF_SSM:OFF_GDN], rows,
                                 ssm_conv_w[l], ssm_conv_b[l], ssm_dt_bias[l], ssm_A_log[l], ssm_D[l], ssm_norm_w[l])
        gdn_c, gdn_l = gdn_mixer(pc[..., OFF_GDN:OFF_GATE], pl[..., OFF_GDN:OFF_GATE], rows,
                                 gdn_conv_w[l], gdn_dt_bias[l], gdn_A_log[l], gdn_norm_w[l])
        hy_l = hyena_mixer(pl[..., :OFF_SSM], rows, hy_conv_w[l], hy_conv_b[l], hy_w1[l], hy_b1[l],
                           hy_w2[l], hy_b2[l], hy_w3[l], hy_freq[l], hy_bias[l])
        xl = xl + mod_l[:, 2] * branch_merge(pl[..., OFF_GATE:], hy_l, ssm_l, gdn_l,
                                             w_hy_out[l], w_ssm_out[l], w_gdn_out[l], w_out[l])
        xl = xl + mod_l[:, 5] * swiglu(rmsnorm(xl, norm2_w[l]) * (1 + mod_l[:, 4]) + mod_l[:, 3],
                                       w_gate_up[l], w_down[l])

        if not last:
            hy_c = hyena_mixer(pc[..., :OFF_SSM], 1, hy_conv_w[l], hy_conv_b[l], hy_w1[l], hy_b1[l],
                               hy_w2[l], hy_b2[l], hy_w3[l], hy_freq[l], hy_bias[l])
            xc = xc + mod_c[2] * branch_merge(pc[..., OFF_GATE:], hy_c, ssm_c, gdn_c,
                                              w_hy_out[l], w_ssm_out[l], w_gdn_out[l], w_out[l])
            xc = xc + mod_c[5] * swiglu(rmsnorm(xc, norm2_w[l]) * (1 + mod_c[4]) + mod_c[3],
                                        w_gate_up[l], w_down[l])
    return rmsnorm(xl, final_norm_w)
```

```python
import contextlib
import numpy as np
import ml_dtypes
import concourse.bass as bass
import concourse.mybir as mybir
from concourse.bass_utils import run_bass_kernel_spmd

F32 = mybir.dt.float32
BF16 = mybir.dt.bfloat16
AF = mybir.ActivationFunctionType
ALU = mybir.AluOpType
AX = mybir.AxisListType

COMPUTE = ("pe", "act", "dve", "pool")
DMAQ = ("sp", "pool", "act")
NEPOCH = 10
EPOCH_MAX = 24000
DK = 6


class Prog:
    def __init__(self, nc):
        self.nc = nc
        self.es = contextlib.ExitStack()
        self.ops = {e: [] for e in ("pe", "act", "dve", "pool", "sp")}
        self.cnt = {e: 0 for e in COMPUTE}
        self.epoch = {e: 0 for e in COMPUTE}
        self.csem = {e: [self.es.enter_context(nc.semaphore(f"s_{e}_{i}")) for i in range(NEPOCH)] for e in COMPUTE}
        self.dsem = {q: [self.es.enter_context(nc.semaphore(f"d_{q}_{i}")) for i in range(DK)] for q in DMAQ}
        self.dcnt = {q: 0 for q in DMAQ}
        self.known_c = {e: {} for e in self.ops}
        self.known_d = {e: set() for e in self.ops}
        self.buf = {}
        self.n_alloc = 0
        self.n_inst = 0
        self.out_events = []

    def sb(self, shape, dtype, name=None):
        self.n_alloc += 1
        name = name or f"t{self.n_alloc}"
        return self.es.enter_context(self.nc.sbuf_tensor(f"{name}_{self.n_alloc}", list(shape), dtype))

    def ps(self, shape, dtype, name=None):
        self.n_alloc += 1
        name = name or f"p{self.n_alloc}"
        return self.es.enter_context(self.nc.psum_tensor(f"{name}_{self.n_alloc}", list(shape), dtype))

    def _wait(self, eng, ev):
        if ev[0] == "c":
            _, e2, ep, c = ev
            if eng == "pe" and e2 == "pe":
                return
            k = self.known_c[eng]
            if k.get((e2, ep), 0) >= c:
                return
            k[(e2, ep)] = c
            sem = self.csem[e2][ep]
            self.ops[eng].append(lambda e, sem=sem, c=c: e.wait_ge(sem, c))
        else:
            _, q, idx = ev
            if ev in self.known_d[eng]:
                return
            self.known_d[eng].add(ev)
            sem = self.dsem[q][idx % DK]
            val = 16 * (idx // DK + 1)
            self.ops[eng].append(lambda e, sem=sem, val=val: e.wait_ge(sem, val))

    def emit(self, eng, fn, reads=(), writes=(), dma=False):
        deps = []
        for b in reads:
            st = self.buf.get(b)
            if st and st[0] is not None:
                deps.append(st[0])
            if st and isinstance(b, str) and b.startswith("psum"):
                deps.extend(ev_ for e_, ev_ in st[1].items() if e_ != eng)
        for b in writes:
            st = self.buf.get(b)
            if st:
                if st[0] is not None:
                    deps.append(st[0])
                deps.extend(st[1].values())
                deps.extend(st[2])
        for ev in deps:
            self._wait(eng, ev)
        self.n_inst += 1
        if dma:
            q = eng
            idx = self.dcnt[q]
            self.dcnt[q] += 1
            if idx >= DK:
                self._wait(eng, ("d", q, idx - DK))
            sem = self.dsem[q][idx % DK]
            self.ops[eng].append(lambda e, fn=fn, sem=sem: fn(e).then_inc(sem, 16))
            ev = ("d", q, idx)
        else:
            if self.cnt[eng] >= EPOCH_MAX:
                self.epoch[eng] += 1
                self.cnt[eng] = 0
                assert self.epoch[eng] < NEPOCH
            self.cnt[eng] += 1
            ep = self.epoch[eng]
            c = self.cnt[eng]
            sem = self.csem[eng][ep]
            self.ops[eng].append(lambda e, fn=fn, sem=sem: fn(e).then_inc(sem, 1))
            ev = ("c", eng, ep, c)
        for b in writes:
            self.buf[b] = [ev, {}, []]
        for b in reads:
            st = self.buf.setdefault(b, [None, {}, []])
            if ev[0] == "c":
                st[1][ev[1]] = ev
            else:
                st[2].append(ev)
        return ev

    def finish(self):
        for ev in self.out_events:
            self._wait("sp", ev)

    def build(self):
        nc = self.nc
        ops = self.ops
        with nc.Block() as block:
            @block.tensor
            def _(e):
                for f in ops["pe"]:
                    f(e)

            @block.scalar
            def _(e):
                for f in ops["act"]:
                    f(e)

            @block.vector
            def _(e):
                for f in ops["dve"]:
                    f(e)

            @block.gpsimd
            def _(e):
                for f in ops["pool"]:
                    f(e)

            @block.sync
            def _(e):
                for f in ops["sp"]:
                    f(e)
        self.es.close()

    def mm(self, out, lhsT, rhs, start, stop, rd, wr):
        self.emit("pe", lambda e: e.matmul(out, lhsT, rhs, start=start, stop=stop), rd, wr)

    def tr(self, out, in_, ident, rd, wr):
        self.emit("pe", lambda e: e.transpose(out, in_, ident), rd, wr)

    def act(self, out, in_, func, rd, wr, bias=None, scale=None):
        kw = {}
        if bias is not None:
            kw["bias"] = bias
        if scale is not None:
            kw["scale"] = scale
        self.emit("act", lambda e: e.activation(out, in_, func, **kw), rd, wr)

    def tt(self, out, in0, in1, op, rd, wr, eng="dve"):
        self.emit(eng, lambda e: e.tensor_tensor(out, in0, in1, op), rd, wr)

    def ts(self, out, in0, s1, s2, op0, op1, rd, wr, eng="dve"):
        if op1 is None:
            self.emit(eng, lambda e: e.tensor_scalar(out, in0, s1, None, op0), rd, wr)
        else:
            self.emit(eng, lambda e: e.tensor_scalar(out, in0, s1, s2, op0, op1), rd, wr)

    def stt(self, out, in0, scalar, in1, op0, op1, rd, wr, eng="dve"):
        self.emit(eng, lambda e: e.scalar_tensor_tensor(out, in0, scalar, in1, op0, op1), rd, wr)

    def cp(self, out, in_, rd, wr, eng="dve"):
        if eng == "act":
            self.emit("act", lambda e: e.copy(out, in_), rd, wr)
        else:
            self.emit(eng, lambda e: e.tensor_copy(out, in_), rd, wr)

    def memset(self, ap, val, wr, eng="dve"):
        self.emit(eng, lambda e: e.memset(ap, val), (), wr)

    def dma(self, out, in_, rd, wr, q="sp", is_out=False):
        ev = self.emit(q, lambda e: e.dma_start(out=out, in_=in_), rd, wr, dma=True)
        if is_out:
            self.out_events.append(ev)
        return ev


class Rot:
    def __init__(self, P, shape, dtype, n, name, psum=False):
        self.t = [(P.ps if psum else P.sb)(shape, dtype, f"{name}{i}") for i in range(n)]
        self.keys = [f"{name}#{i}" for i in range(n)]
        self.i = 0

    def next(self):
        i = self.i % len(self.t)
        self.i += 1
        return self.t[i], self.keys[i]


import math

D = 1024
T = 2304
NCTX = 256
NLAT = 2048
DFF = 2816
NL = 4
EPS = 1e-6
ADA_INTERLEAVE = False
TT = [(0, 256, 1), (256, 512, 0), (768, 512, 0), (1280, 512, 0), (1792, 512, 0)]
ARENA_BYTES = 207 * 1024


def _prod(s):
    r = 1
    for v in s:
        r *= v
    return r


class KProg(Prog):
    def __init__(self, nc):
        super().__init__(nc)
        self.arena = self.es.enter_context(nc.sbuf_tensor("arena", [128, ARENA_BYTES // 2], BF16))
        self.arena_f = self.arena.bitcast(F32)
        self.sp_ = 0
        self.stack = []
        self.psum = [self.es.enter_context(nc.psum_tensor(f"psb{i}", [128, 512], F32)) for i in range(8)]
        self.psum_bf = [p.bitcast(BF16) for p in self.psum]
        self.ps_i = 0

    def sb(self, shape, dtype, name=None):
        shape = list(shape)
        n = _prod(shape[1:])
        esz = 4 if dtype == F32 else 2
        size = (n * esz + 63) // 64 * 64
        off = self.sp_
        self.sp_ += size
        assert self.sp_ <= ARENA_BYTES, f"SBUF arena overflow {self.sp_} ({name})"
        if dtype == F32:
            base = self.arena_f[0:shape[0], off // 4: off // 4 + n]
        else:
            base = self.arena[0:shape[0], off // 2: off // 2 + n]
        if len(shape) > 2:
            names = [f"d{i}" for i in range(len(shape) - 1)]
            pat = "p (" + " ".join(names) + ") -> p " + " ".join(names)
            base = base.rearrange(pat, **{nm: s for nm, s in zip(names[:-1], shape[1:-1])})
        return base

    def push(self):
        self.stack.append(self.sp_)

    def pop(self):
        self.barrier()
        self.sp_ = self.stack.pop()

    def barrier(self):
        for E in self.ops:
            for e2 in COMPUTE:
                if e2 != E and self.cnt[e2] > 0:
                    self._wait(E, ("c", e2, self.epoch[e2], self.cnt[e2]))
            for q in DMAQ:
                for idx in range(max(0, self.dcnt[q] - DK), self.dcnt[q]):
                    self._wait(E, ("d", q, idx))

    def pbank(self, bf=False):
        i = self.ps_i % 8
        self.ps_i += 1
        return (self.psum_bf[i][:, 0:1024] if bf else self.psum[i][:, 0:512]), f"psum{i}"


class SRot:
    def __init__(self, P, shape, dtype, n, name):
        self.t = [P.sb(shape, dtype, name) for _ in range(n)]
        self.keys = [f"{name}#{i}" for i in range(n)]
        self.i = 0

    def next(self):
        i = self.i % len(self.t)
        self.i += 1
        return self.t[i], self.keys[i]


MAIN_COLS = list(range(0, 3072)) + list(range(3088, 5136)) + list(range(5152, 8224))
SMALL_COLS = list(range(3072, 3088)) + list(range(5136, 5152))
NCST = 8


def _chunked_w(w, ncol_chunk=128):
    K, N = w.shape
    return np.ascontiguousarray(w.reshape(K // 128, 128, N // ncol_chunk, ncol_chunk).transpose(2, 1, 0, 3))


def _pvec(v):
    sh = v.shape
    a = v.reshape(*sh[:-1], sh[-1] // 128, 128)
    return np.ascontiguousarray(np.moveaxis(a, -1, 0))


def host_consts():
    c = np.zeros((128, NCST, 128), np.float32)
    k = np.arange(128)[:, None]
    j = np.arange(128)[None, :]
    c[:, 0, :] = (k == j)
    c[:, 1, :] = 1.0
    c[:, 2, :] = (k <= j)
    c[:, 3, :] = (k >= j)
    c[:, 4, :] = (k < j)
    c[:, 5, :] = (k > j)
    return c


def _bf16(a):
    return np.ascontiguousarray(a.astype(ml_dtypes.bfloat16))


def hyena_tables(L):
    nt = L // 128
    t = np.arange(L, dtype=np.float64)[:, None]
    w = 2.0 * np.pi * (np.arange(L, dtype=np.float64)[None, :] + 0.5) / (2 * L)
    C = np.cos(t * w)
    S = np.sin(t * w)
    def fwd(M):
        return M.reshape(nt, 128, nt, 128).transpose(2, 1, 0, 3)
    fw = np.stack([fwd(C), fwd(S)])
    def inv(M):
        return (M.T / L).reshape(nt, 128, nt, 128).transpose(2, 1, 0, 3)
    iv = np.concatenate([inv(C), inv(S)], axis=2)
    tt_ = np.linspace(0.0, 1.0, L, dtype=np.float32)[:, None]
    ww = (2.0 * math.pi * np.arange(L, dtype=np.float32)[:, None] / L).astype(np.float32)
    ff = np.linspace(1e-4, 15, 16, dtype=np.float32)[None, :]
    z = np.concatenate([tt_, np.cos(ff * ww), -np.sin(ff * ww)], axis=-1).astype(np.float32)
    min_decay = math.log(1e-2) / 1.5
    max_decay = math.log(1e-2) / 0.3
    deltas = np.linspace(min_decay, max_decay, 512, dtype=np.float32)
    win = np.exp(-tt_ * np.abs(deltas)[None, :]).astype(np.float32)
    win1 = win.copy()
    win1[0, :] = 0.0
    wn = np.stack([win, win1]).reshape(2, nt, 128, 512).transpose(0, 2, 1, 3)
    return _bf16(fw), _bf16(iv), np.ascontiguousarray(z.T), np.ascontiguousarray(wn.astype(np.float32))


def prep_shared(inp):
    f = lambda a: np.ascontiguousarray(np.asarray(a, dtype=np.float32))
    sh = {}
    sh["wada"] = np.stack([_chunked_w(f(inp["w_ada"][l])) for l in range(NL)])
    sh["bada"] = _pvec(f(inp["b_ada"]))
    sh["n1w"] = _pvec(f(inp["norm1_w"]))
    sh["n2w"] = _pvec(f(inp["norm2_w"]))
    sh["fnw"] = _pvec(f(inp["final_norm_w"]))
    win = f(inp["w_in"])
    sh["win"] = np.stack([_chunked_w(win[l][:, MAIN_COLS]) for l in range(NL)])
    sh["wsm"] = np.ascontiguousarray(win[:, :, SMALL_COLS].reshape(NL, 8, 128, 32).transpose(0, 2, 1, 3))
    sh["hycw"] = np.ascontiguousarray(_pvec(f(inp["hy_conv_w"])).transpose(0, 1, 3, 2))
    sh["hycb"] = _pvec(f(inp["hy_conv_b"]))
    sh["sscw"] = np.ascontiguousarray(_pvec(f(inp["ssm_conv_w"])).transpose(0, 1, 3, 2))
    sh["sscb"] = _pvec(f(inp["ssm_conv_b"]))
    sh["gdcw"] = np.ascontiguousarray(_pvec(f(inp["gdn_conv_w"])).transpose(0, 1, 3, 2))
    wb = np.stack([f(inp["w_hy_out"]), f(inp["w_ssm_out"]), f(inp["w_gdn_out"])], 1)
    wb = wb.reshape(NL, 3, 4, 128, 8, 128)
    sh["wbr"] = np.ascontiguousarray(wb.transpose(0, 4, 3, 1, 2, 5))
    sh["wout"] = np.stack([_chunked_w(f(inp["w_out"][l])) for l in range(NL)])
    sh["wgu"] = np.stack([_chunked_w(f(inp["w_gate_up"][l])) for l in range(NL)])
    sh["wdn"] = np.stack([_chunked_w(f(inp["w_down"][l])) for l in range(NL)])
    sh["cst"] = host_consts()
    bc = lambda a: np.ascontiguousarray(np.broadcast_to(a[None], (128,) + a.shape))
    sh["ssdtb"] = bc(f(inp["ssm_dt_bias"]).reshape(NL, 16))
    sh["ssAl"] = bc(f(inp["ssm_A_log"]).reshape(NL, 16))
    Dch = np.repeat(f(inp["ssm_D"]), 64, axis=1)
    sh["ssDpc"] = _pvec(Dch)
    sh["ssnw"] = _pvec(f(inp["ssm_norm_w"]))
    for nm, L in (("lat", NLAT), ("ctx", NCTX)):
        fw, iv, zT, wn = hyena_tables(L)
        sh["hfw_" + nm], sh["hiv_" + nm], sh["hz_" + nm], sh["hwin_" + nm] = fw, iv, zT, wn
    sh["hyw1"] = f(inp["hy_w1"])
    sh["hyw2"] = f(inp["hy_w2"])
    sh["hyw3"] = f(inp["hy_w3"])
    sh["hyb"] = np.ascontiguousarray(np.stack([f(inp["hy_b1"]), f(inp["hy_b2"])], -1))
    sh["hyfreq"] = np.ascontiguousarray(f(inp["hy_freq"]).transpose(0, 2, 1))
    sh["hybias"] = bc(f(inp["hy_bias"]))
    sh["gddtb"] = bc(f(inp["gdn_dt_bias"]).reshape(NL, 8))
    sh["gdAl"] = bc(f(inp["gdn_A_log"]).reshape(NL, 8))
    sh["gdnw"] = np.ascontiguousarray(f(inp["gdn_norm_w"]).T)
    return sh


def prep_core(inp, b):
    xin = np.concatenate([np.asarray(inp["ctx"][b], np.float32), np.asarray(inp["x"][b], np.float32)], 0)
    xT0 = np.ascontiguousarray(xin.T.reshape(8, 128, T).transpose(1, 0, 2))
    cT = np.stack([np.asarray(inp["c"][b], np.float32).reshape(8, 128).T,
                   np.asarray(inp["c_ctx"], np.float32).reshape(8, 128).T], -1)
    return {"xT0": xT0, "cT": np.ascontiguousarray(cT)}


def build_program(nlayers=NL, dbg=False, mixers=("hy", "ssm", "gdn")):
    nc = bass.Bass("TRN2", target_bir_lowering=False)

    def IN(name, shape, dt=F32):
        return nc.dram_tensor(name, list(shape), dt, kind="ExternalInput").ap()

    def SCR(name, shape, dt):
        return nc.dram_tensor(name, list(shape), dt, kind="Internal").ap()

    def OUT(name, shape, dt=F32):
        return nc.dram_tensor(name, list(shape), dt, kind="ExternalOutput").ap()

    d_xT0 = IN("xT0", [128, 8, T])
    d_cT = IN("cT", [128, 8, 2])
    d_wada = IN("wada", [NL, 48, 128, 8, 128])
    d_bada = IN("bada", [128, NL, 48])
    d_n1w = IN("n1w", [128, NL, 8])
    d_n2w = IN("n2w", [128, NL, 8])
    d_fnw = IN("fnw", [128, 8])
    d_win = IN("win", [NL, 64, 128, 8, 128])
    d_wsm = IN("wsm", [NL, 128, 8, 32])
    d_hycw = IN("hycw", [128, NL, 12, 3])
    d_hycb = IN("hycb", [128, NL, 12])
    d_sscw = IN("sscw", [128, NL, 8, 3])
    d_sscb = IN("sscb", [128, NL, 8])
    d_gdcw = IN("gdcw", [128, NL, 12, 3])
    d_wbr = IN("wbr", [NL, 8, 128, 3, 4, 128])
    d_wout = IN("wout", [NL, 8, 128, 8, 128])
    d_wgu = IN("wgu", [NL, 44, 128, 8, 128])
    d_wdn = IN("wdn", [NL, 8, 128, 22, 128])
    d_cst = IN("cst", [128, NCST, 128])
    d_ssdtb = IN("ssdtb", [128, NL, 16])
    d_ssAl = IN("ssAl", [128, NL, 16])
    d_ssDpc = IN("ssDpc", [128, NL, 4])
    d_ssnw = IN("ssnw", [128, NL, 4])
    d_gddtb = IN("gddtb", [128, NL, 8])
    HY = {}
    for nm, L in (("lat", NLAT), ("ctx", NCTX)):
        nt_ = L // 128
        HY[nm] = dict(L=L, nt=nt_,
                      fw=IN("hfw_" + nm, [2, nt_, 128, nt_, 128], BF16),
                      iv=IN("hiv_" + nm, [nt_, 128, 2 * nt_, 128], BF16),
                      z=IN("hz_" + nm, [33, L]),
                      win=IN("hwin_" + nm, [2, 128, nt_, 512]),
                      K=SCR("s_K_" + nm, [2, 2, nt_, 128, 512], F32),
                      tile0=(2 if nm == "lat" else 0))
    d_hyw1 = IN("hyw1", [NL, 33, 64])
    d_hyw2 = IN("hyw2", [NL, 64, 64])
    d_hyw3 = IN("hyw3", [NL, 64, 2048])
    d_hyb = IN("hyb", [NL, 64, 2])
    d_hyfreq = IN("hyfreq", [NL, 64, 2])
    d_hybias = IN("hybias", [128, NL, 2, 512])
    d_gdAl = IN("gdAl", [128, NL, 8])
    d_gdnw = IN("gdnw", [128, NL])
    s_go = SCR("s_go", [2, 36, 64, 512], F32)
    d_outT = OUT("outT", [128, 8, NLAT])

    s_hy = SCR("s_hy", [12, 128, T], F32)
    s_sx = SCR("s_sx", [4, 128, T], F32)
    s_sbc = SCR("s_sbc", [4, 128, T], BF16)
    s_sz = SCR("s_sz", [4, 128, T], BF16)
    s_gq = SCR("s_gq", [4, 128, T], BF16)
    s_gk = SCR("s_gk", [4, 128, T], BF16)
    s_gv = SCR("s_gv", [4, 128, T], BF16)
    s_gg = SCR("s_gg", [4, 128, T], BF16)
    s_gate = SCR("s_gate", [24, 128, T], BF16)
    s_small = SCR("s_small", [32, T], F32)
    s_y = SCR("s_y", [12, 128, T], BF16)

    dbg_out = {}

    def DBG(name, shape, dt=F32):
        dbg_out[name] = OUT(name, shape, dt)
        return dbg_out[name]

    P = KProg(nc)
    mm, act, tt, ts, stt, cp, dma = P.mm, P.act, P.tt, P.ts, P.stt, P.cp, P.dma

    cst = P.sb([128, NCST, 128], F32, "cst")
    ident = cst[:, 0, :]
    ones = cst[:, 1, :]
    cstb = P.sb([128, 2, 128], BF16, "cstb")
    identb = cstb[:, 0, :]
    onesb = cstb[:, 1, :]
    xT = P.sb([128, 8, T], F32, "xT")
    modT = P.sb([128, NL, 48, 2], F32, "modT")
    g1 = P.sb([128, NL, 8, 2], F32, "g1")
    g2 = P.sb([128, NL, 8, 2], F32, "g2")
    sT = P.sb([128, 8, 2], F32, "sT")
    bada = P.sb([128, NL, 48], F32, "bada")
    n1w = P.sb([128, NL, 8], F32, "n1w")
    n2w = P.sb([128, NL, 8], F32, "n2w")
    fnw = P.sb([128, 8], F32, "fnw")
    hycw = P.sb([128, NL, 12, 3], F32, "hycw")
    hycb = P.sb([128, NL, 12], F32, "hycb")
    sscw = P.sb([128, NL, 8, 3], F32, "sscw")
    sscb = P.sb([128, NL, 8], F32, "sscb")
    gdcw = P.sb([128, NL, 12, 3], F32, "gdcw")
    ssdtb = P.sb([128, NL, 16], F32, "ssdtb")
    ssA = P.sb([128, NL, 16], F32, "ssA")
    ssDpc = P.sb([128, NL, 4], F32, "ssDpc")
    ssnw = P.sb([128, NL, 4], F32, "ssnw")
    gddtb = P.sb([128, NL, 8], F32, "gddtb")
    gdA = P.sb([128, NL, 8], F32, "gdA")
    gdnw = P.sb([128, NL], F32, "gdnw")
    gmask = P.sb([64, 4, 4, 64], F32, "gmask")
    gI4 = P.sb([64, 4, 64], F32, "gI4")

    dma(cst, d_cst, [], ["cst"])
    cp(cstb, cst[:, 0:2, :], ["cst"], ["cstb"])
    for c in range(8):
        dma(xT[:, c, :], d_xT0[:, c, :], [], [("xT", c)])
    dma(sT, d_cT, [], ["sT"])
    for nm, sbt, dt_ in (("bada", bada, d_bada), ("n1w", n1w, d_n1w), ("n2w", n2w, d_n2w), ("fnw", fnw, d_fnw),
                         ("hycw", hycw, d_hycw), ("hycb", hycb, d_hycb), ("sscw", sscw, d_sscw),
                         ("sscb", sscb, d_sscb), ("gdcw", gdcw, d_gdcw), ("ssdtb", ssdtb, d_ssdtb),
                         ("ssA", ssA, d_ssAl), ("ssDpc", ssDpc, d_ssDpc), ("ssnw", ssnw, d_ssnw),
                         ("gddtb", gddtb, d_gddtb), ("gdA", gdA, d_gdAl), ("gdnw", gdnw, d_gdnw)):
        dma(sbt, dt_, [], [nm])
    act(gdA, gdA, AF.Exp, ["gdA"], ["gdA"])
    ts(gdA, gdA, -1.0, None, ALU.mult, None, ["gdA"], ["gdA"])
    for mi, ci in enumerate((5, 4, 2, 3)):
        for h in range(4):
            cp(gmask[:, mi, h, :], cst[0:64, ci, 0:64], ["cst"], ["gmask"])
    for h in range(4):
        cp(gI4[:, h, :], cst[0:64, 0, 0:64], ["cst"], ["gI4"])
    act(ssA, ssA, AF.Exp, ["ssA"], ["ssA"])
    ts(ssA, ssA, -1.0, None, ALU.mult, None, ["ssA"], ["ssA"])
    act(sT, sT, AF.Silu, ["sT"], ["sT"])

    def ada_slab(l, j, slab_rot, q):
        slab, sk = slab_rot.next()
        dma(slab, d_wada[l, j], [], [sk], q=q)
        ps, pk = P.pbank()
        for kc in range(8):
            mm(ps[:, 0:2], slab[:, kc, :], sT[:, kc, :], kc == 0, kc == 7, [sk, "sT"], [pk])
        ts(modT[:, l, j, :], ps[:, 0:2], bada[:, l, j:j + 1], None, ALU.add, None, [pk, "bada"], [("modT", l)])

    def ada_finish(l):
        for w in range(2):
            ts(g1[:, l, :, w], modT[:, l, 8:16, w], 1.0, None, ALU.add, None, [("modT", l)], [("g1", l)])
            tt(g1[:, l, :, w], g1[:, l, :, w], n1w[:, l, :], ALU.mult, [("g1", l), "n1w"], [("g1", l)])
            ts(g2[:, l, :, w], modT[:, l, 32:40, w], 1.0, None, ALU.add, None, [("modT", l)], [("g2", l)])
            tt(g2[:, l, :, w], g2[:, l, :, w], n2w[:, l, :], ALU.mult, [("g2", l), "n2w"], [("g2", l)])

    P.push()
    slab_rot0 = SRot(P, [128, 8, 128], F32, 3, "adaslab")
    for l_ in range(nlayers):
        for j in range(48):
            ada_slab(l_, j, slab_rot0, "sp" if j % 2 == 0 else "act")
        ada_finish(l_)
    P.pop()
    if dbg:
        dma(DBG("dbg_mod", [128, NL, 48, 2]), modT, [("modT", l_) for l_ in range(NL)], ["dbg_mod"], is_out=True)

    def norm_mod(hT, gsel, shsel, tiles=TT, out_key="hT"):
        sq_rot = SRot(P, [128, 512], BF16, 3, "nsq")
        tmp_rot = SRot(P, [128, 512], F32, 3, "ntmp")
        rstd_rot = SRot(P, [128, 512], F32, 2, "nrstd")
        for (t0, n, w) in tiles:
            ps, pk = P.pbank()
            for c in range(8):
                sq, sqk = sq_rot.next()
                act(sq[:, :n], xT[:, c, t0:t0 + n], AF.Square, [("xT", c)], [sqk])
                mm(ps[:, :n], onesb, sq[:, :n], c == 0, c == 7, ["cstb", sqk], [pk])
            rstd, rk = rstd_rot.next()
            ts(rstd[:, :n], ps[:, :n], 1.0 / D, EPS, ALU.mult, ALU.add, [pk], [rk])
            act(rstd[:, :n], rstd[:, :n], AF.Sqrt, [rk], [rk])
            P.emit("dve", lambda e, o=rstd[:, :n]: e.reciprocal(o, o), [rk], [rk])
            for c in range(8):
                tmp, tk = tmp_rot.next()
                stt(tmp[:, :n], xT[:, c, t0:t0 + n], gsel(c, w), rstd[:, :n], ALU.mult, ALU.mult,
                    [("xT", c), rk] + [(nm_, l_) for nm_ in ("g1", "g2") for l_ in range(NL)] + ["fnw"], [tk])
                if shsel is not None:
                    act(hT[:, c, t0:t0 + n], tmp[:, :n], AF.Identity, [tk] + [("modT", l_) for l_ in range(NL)], [(out_key, c)], bias=shsel(c, w))
                else:
                    act(hT[:, c, t0:t0 + n], tmp[:, :n], AF.Copy, [tk], [(out_key, c)])

    def conv3(dst, src, w3, rows_lat=32):
        ops_ = []
        segs = [(src[:, 0:NCTX].rearrange("p (r t) -> p r t", r=1), dst[:, 0:NCTX].rearrange("p (r t) -> p r t", r=1), NCTX),
                (src[:, NCTX:T].rearrange("p (r t) -> p r t", r=rows_lat), dst[:, NCTX:T].rearrange("p (r t) -> p r t", r=rows_lat), 64)]
        for s3, d3, rl in segs:
            ops_.append(("ts", d3, s3, w3[:, 1:2]))
            ops_.append(("stt", d3[:, :, 1:rl], s3[:, :, 0:rl - 1], w3[:, 0:1], d3[:, :, 1:rl]))
            ops_.append(("stt", d3[:, :, 0:rl - 1], s3[:, :, 1:rl], w3[:, 2:3], d3[:, :, 0:rl - 1]))
        return ops_

    def run_conv(dst, dk_, src, sk_, w3, wkey, eng="dve"):
        for op in conv3(dst, src, w3):
            if op[0] == "ts":
                ts(op[1], op[2], op[3], None, ALU.mult, None, [sk_, wkey], [dk_], eng=eng)
            else:
                stt(op[1], op[2], op[3], op[4], ALU.mult, ALU.add, [sk_, wkey, dk_], [dk_], eng=eng)

    def layer(l):
        P.push()
        hT = P.sb([128, 8, T], BF16, "hT")
        P.push()
        norm_mod(hT, lambda c, w: g1[:, l, c, w:w + 1], lambda c, w: modT[:, l, c, w:w + 1])
        P.pop()
        hkeys = [("hT", c) for c in range(8)]
        P.push()
        wslab_rot = SRot(P, [128, 8, 128], BF16, 3, "wslab")
        pc_rot = SRot(P, [128, T], F32, 2, "pc")
        cv_rot = SRot(P, [128, T], F32, 2, "cv")
        ob_rot = SRot(P, [128, T], BF16, 2, "ob")
        l2_rot = SRot(P, [128, 512], F32, 2, "l2t")
        ada_rot = SRot(P, [128, 8, 128], F32, 3, "adaslab2")
        ada_next = [0]

        def ada_step():
            if ADA_INTERLEAVE and l + 1 < nlayers and ada_next[0] < 48:
                ada_slab(l + 1, ada_next[0], ada_rot, "sp")
                ada_next[0] += 1
                if ada_next[0] == 48:
                    ada_finish(l + 1)

        def proj(j, consumer, ncols=128, small=False):
            slab, sk = wslab_rot.next()
            if small:
                dma(slab[:, :, 0:32], d_wsm[l], [], [sk], q="pool")
            else:
                dma(slab, d_win[l, j], [], [sk], q="pool")
            for (t0, n, w) in TT:
                ps, pk = P.pbank()
                for kc in range(8):
                    mm(ps[0:ncols, :n], slab[:, kc, 0:ncols], hT[:, kc, t0:t0 + n], kc == 0, kc == 7, [sk, hkeys[kc]], [pk])
                consumer(ps, pk, t0, n)
            ada_step()

        def conv_chunk(j, w3, wkey, bias, bkey, func, dst_dram, dst_f32, post=None):
            pc, pck = pc_rot.next()
            proj(j, lambda ps, pk, t0, n: act(pc[:, t0:t0 + n], ps[:, :n], AF.Copy, [pk], [pck]))
            cv, cvk = cv_rot.next()
            run_conv(cv, cvk, pc, pck, w3, wkey)
            if post is not None:
                post(cv, cvk, dst_dram)
                return
            if dst_f32:
                kw = {"bias": bias} if bias is not None else {}
                act(cv, cv, func, [cvk, bkey], [cvk], **kw)
                dma(dst_dram, cv, [cvk], [("scr", str(dst_dram))])
            else:
                ob, obk = ob_rot.next()
                kw = {"bias": bias} if bias is not None else {}
                act(ob, cv, func, [cvk, bkey], [obk], **kw)
                dma(dst_dram, ob, [obk], [("scr", str(dst_dram))])

        def plain_chunk(j, func, dst_dram):
            ob, obk = ob_rot.next()
            proj(j, lambda ps, pk, t0, n: act(ob[:, t0:t0 + n], ps[:, :n], func, [pk], [obk]))
            dma(dst_dram, ob, [obk], [("scr", str(dst_dram))])

        def l2norm_post(scale):
            def post(cv, cvk, dst_dram):
                act(cv, cv, AF.Silu, [cvk], [cvk])
                ob, obk = ob_rot.next()
                for (t0, n, w) in TT:
                    sq, sqk = l2_rot.next()
                    act(sq[:, :n], cv[:, t0:t0 + n], AF.Square, [cvk], [sqk])
                    ps, pk = P.pbank()
                    mm(ps[:, :n], ones, sq[:, :n], True, True, ["cst", sqk], [pk])
                    ts(sq[:, :n], ps[:, :n], EPS, None, ALU.add, None, [pk], [sqk])
                    act(sq[:, :n], sq[:, :n], AF.Sqrt, [sqk], [sqk])
                    P.emit("dve", lambda e, o=sq[:, :n]: e.reciprocal(o, o), [sqk], [sqk])
                    stt(ob[:, t0:t0 + n], cv[:, t0:t0 + n], scale, sq[:, :n], ALU.mult, ALU.mult, [cvk, sqk], [obk])
                dma(dst_dram, ob, [obk], [("scr", str(dst_dram))])
            return post

        for j in range(12):
            conv_chunk(j, hycw[:, l, j, :], "hycw", hycb[:, l, j:j + 1], "hycb", AF.Identity, s_hy[j], True)
        for j in range(4):
            plain_chunk(12 + j, AF.Silu, s_sz[j])
        for j in range(8):
            if j < 4:
                conv_chunk(16 + j, sscw[:, l, j, :], "sscw", sscb[:, l, j:j + 1], "sscb", AF.Silu, s_sx[j], True)
            else:
                conv_chunk(16 + j, sscw[:, l, j, :], "sscw", sscb[:, l, j:j + 1], "sscb", AF.Silu, s_sbc[j - 4], False)
        for j in range(12):
            if j < 4:
                conv_chunk(24 + j, gdcw[:, l, j, :], "gdcw", None, "gdcw", None, s_gq[j], False, post=l2norm_post(128.0 ** -0.5))
            elif j < 8:
                conv_chunk(24 + j, gdcw[:, l, j, :], "gdcw", None, "gdcw", None, s_gk[j - 4], False, post=l2norm_post(1.0))
            else:
                conv_chunk(24 + j, gdcw[:, l, j, :], "gdcw", None, "gdcw", AF.Silu, s_gv[j - 8], False)
        for j in range(4):
            plain_chunk(36 + j, AF.Silu, s_gg[j])
        for j in range(24):
            plain_chunk(40 + j, AF.Sigmoid, s_gate[j])
        sm, smk = pc_rot.next()
        proj(0, lambda ps, pk, t0, n: act(sm[0:32, t0:t0 + n], ps[0:32, :n], AF.Copy, [pk], [smk]), ncols=32, small=True)
        dma(s_small, sm[0:32, :], [smk], [("scr", str(s_small))])
        while ADA_INTERLEAVE and l + 1 < nlayers and ada_next[0] < 48:
            ada_step()
        P.pop()
        P.pop()

        if dbg and l == 0:
            P.push()
            for nm, src, nchunk, dt_ in (("dbg_hy", s_hy, 12, F32), ("dbg_sx", s_sx, 4, F32), ("dbg_sbc", s_sbc, 4, BF16),
                                         ("dbg_sz", s_sz, 4, BF16), ("dbg_gq", s_gq, 4, BF16), ("dbg_gk", s_gk, 4, BF16),
                                         ("dbg_gv", s_gv, 4, BF16), ("dbg_gg", s_gg, 4, BF16), ("dbg_gate", s_gate, 24, BF16)):
                o = DBG(nm, [nchunk, 128, T], dt_)
                for j in range(nchunk):
                    dma(o[j], src[j], [("scr", str(src[j]))], [nm + str(j)], is_out=True)
            o = DBG("dbg_small", [32, T], F32)
            dma(o, s_small, [("scr", str(s_small))], ["dbg_small"], is_out=True)
            P.pop()

        if "hy" in mixers:
            hyena(l)
        else:
            for j in range(4):
                dma(s_y[j], s_gk[j], [("scr", str(s_gk[j]))], [("scr", str(s_y[j]))])
        if "ssm" in mixers:
            ssd(l)
        else:
            for j in range(4):
                dma(s_y[4 + j], s_sbc[j], [("scr", str(s_sbc[j]))], [("scr", str(s_y[4 + j]))])
        if "gdn" in mixers:
            gdn(l)
        else:
            for j in range(4):
                dma(s_y[8 + j], s_gv[j], [("scr", str(s_gv[j]))], [("scr", str(s_y[8 + j]))])
        if dbg and l == 0:
            P.push()
            o = DBG("dbg_y", [12, 128, T], BF16)
            for j in range(12):
                dma(o[j], s_y[j], [("scr", str(s_y[j]))], ["dbg_y" + str(j)], is_out=True)
            P.pop()

        TTm = TT[1:] if l == nlayers - 1 else TT
        halves = (TTm[0:len(TTm) - 2], TTm[len(TTm) - 2:])
        P.push()
        yT = P.sb([128, 12, T], BF16, "yT")
        mT = P.sb([128, 8, T], BF16, "mT")
        for j in range(12):
            dma(yT[:, j, :], s_y[j], [("scr", str(s_y[j]))], [("yT", j)])
        P.push()
        wb_rot = SRot(P, [128, 3, 4, 128], BF16, 2, "wbr")
        gt_rot = SRot(P, [128, 3, 512], BF16, 3, "gt")
        acc_rot = SRot(P, [128, 512], F32, 3, "macc")
        for dch in range(8):
            wb, wbk = wb_rot.next()
            dma(wb, d_wbr[l, dch], [], [wbk], q="pool")
            for (t0, n, w) in TTm:
                gt_, gtk = gt_rot.next()
                for i in range(3):
                    dma(gt_[:, i, 0:n], s_gate[8 * i + dch][:, t0:t0 + n], [("scr", str(s_gate[8 * i + dch]))], [gtk])
                acc, ak = acc_rot.next()
                for i in range(3):
                    ps, pk = P.pbank()
                    for kc in range(4):
                        mm(ps[:, :n], wb[:, i, kc, :], yT[:, 4 * i + kc, t0:t0 + n], kc == 0, kc == 3, [wbk, ("yT", 4 * i + kc)], [pk])
                    if i == 0:
                        tt(acc[:, :n], ps[:, :n], gt_[:, i, 0:n], ALU.mult, [pk, gtk], [ak])
                    else:
                        tmp, tk = acc_rot.next()
                        tt(tmp[:, :n], ps[:, :n], gt_[:, i, 0:n], ALU.mult, [pk, gtk], [tk])
                        if i == 1:
                            tt(acc[:, :n], acc[:, :n], tmp[:, :n], ALU.add, [ak, tk], [ak], eng="dve")
                        else:
                            tt(mT[:, dch, t0:t0 + n], acc[:, :n], tmp[:, :n], ALU.add, [ak, tk], [("mT", dch)], eng="dve")
        P.pop()
        wo_rot = SRot(P, [128, 8, 128], BF16, 2, "wo")
        for och in range(8):
            wo, wok = wo_rot.next()
            dma(wo, d_wout[l, och], [], [wok], q="pool")
            for (t0, n, w) in TTm:
                ps, pk = P.pbank()
                for kc in range(8):
                    mm(ps[:, :n], wo[:, kc, :], mT[:, kc, t0:t0 + n], kc == 0, kc == 7, [wok, ("mT", kc)], [pk])
                stt(xT[:, och, t0:t0 + n], ps[:, :n], modT[:, l, 16 + och, w:w + 1], xT[:, och, t0:t0 + n], ALU.mult, ALU.add,
                    [pk, ("modT", l), ("xT", och)], [("xT", och)])
        P.pop()

        P.push()
        hT2 = P.sb([128, 8, T], BF16, "hT2")
        P.push()
        norm_mod(hT2, lambda c, w: g2[:, l, c, w:w + 1], lambda c, w: modT[:, l, 24 + c, w:w + 1], tiles=TTm, out_key="hT2")
        P.pop()
        h2keys = [("hT2", c) for c in range(8)]
        for half in halves:
            P.push()
            h0 = half[0][0]
            hn = sum(tl[1] for tl in half)
            aT = P.sb([128, 22, hn], BF16, "aT")
            P.push()
            wg_rot = SRot(P, [128, 8, 128], BF16, 4, "wgu")
            sg_rot = SRot(P, [128, 512], F32, 3, "sg")
            for j in range(22):
                wg, wgk = wg_rot.next()
                dma(wg, d_wgu[l, j], [], [wgk], q="pool")
                wu, wuk = wg_rot.next()
                dma(wu, d_wgu[l, 22 + j], [], [wuk], q="pool")
                for (t0, n, w) in half:
                    psg, pgk = P.pbank()
                    for kc in range(8):
                        mm(psg[:, :n], wg[:, kc, :], hT2[:, kc, t0:t0 + n], kc == 0, kc == 7, [wgk, h2keys[kc]], [pgk])
                    psu, puk = P.pbank()
                    for kc in range(8):
                        mm(psu[:, :n], wu[:, kc, :], hT2[:, kc, t0:t0 + n], kc == 0, kc == 7, [wuk, h2keys[kc]], [puk])
                    sg, sgk = sg_rot.next()
                    act(sg[:, :n], psg[:, :n], AF.Silu, [pgk], [sgk])
                    tt(aT[:, j, t0 - h0:t0 - h0 + n], sg[:, :n], psu[:, :n], ALU.mult, [sgk, puk], [("aT", j)])
            P.pop()
            wd_rot = SRot(P, [128, 22, 128], BF16, 2, "wdn")
            for och in range(8):
                wd, wdk = wd_rot.next()
                dma(wd, d_wdn[l, och], [], [wdk], q="pool")
                for (t0, n, w) in half:
                    ps, pk = P.pbank()
                    for kc in range(22):
                        mm(ps[:, :n], wd[:, kc, :], aT[:, kc, t0 - h0:t0 - h0 + n], kc == 0, kc == 21, [wdk, ("aT", kc)], [pk])
                    stt(xT[:, och, t0:t0 + n], ps[:, :n], modT[:, l, 40 + och, w:w + 1], xT[:, och, t0:t0 + n], ALU.mult, ALU.add,
                        [pk, ("modT", l), ("xT", och)], [("xT", och)])
            P.pop()
        P.pop()

    def hyena(l):
        raise NotImplementedError

    def ssd(l):
        raise NotImplementedError

    def gdn(l):
        raise NotImplementedError

    MIXER_IMPL = {}

    def hyena_impl(l):
        MAGIC = 12582912.0
        TWO_PI = 2.0 * math.pi
        P.push()
        w1s = P.sb([33, 64], F32, "hw1")
        w2s = P.sb([64, 64], F32, "hw2")
        w3s = P.sb([64, 2048], F32, "hw3")
        hb = P.sb([64, 2], F32, "hb")
        hfr = P.sb([64, 2], F32, "hfr")
        hfb = P.sb([64, 2], F32, "hfb")
        dma(w1s, d_hyw1[l], [], ["hw1"])
        dma(w2s, d_hyw2[l], [], ["hw2"])
        dma(w3s, d_hyw3[l], [], ["hw3"])
        dma(hb, d_hyb[l], [], ["hb"])
        dma(hfr, d_hyfreq[l], [], ["hfr"])
        tt(hfb, hfr, hb, ALU.mult, ["hfr", "hb"], ["hfb"])
        t_rot = SRot(P, [128, 512], F32, 4, "hft")

        def sin_mod(dst, dkey, src, skeys, k, n):
            t1, t1k = t_rot.next()
            t2, t2k = t_rot.next()
            ts(t1[0:64, :n], src, hfr[:, k:k + 1], hfb[:, k:k + 1], ALU.mult, ALU.add, skeys + ["hfr", "hfb"], [t1k])
            ts(t2[0:64, :n], t1[0:64, :n], 1.0 / TWO_PI, MAGIC, ALU.mult, ALU.add, [t1k], [t2k])
            ts(t2[0:64, :n], t2[0:64, :n], MAGIC, -TWO_PI, ALU.subtract, ALU.mult, [t2k], [t2k])
            tt(t1[0:64, :n], t1[0:64, :n], t2[0:64, :n], ALU.add, [t1k, t2k], [t1k])
            ts(t1[0:64, :n], t1[0:64, :n], math.pi, -math.pi, ALU.min, ALU.max, [t1k], [t1k])
            act(dst, t1[0:64, :n], AF.Sin, [t1k], [dkey])

        hy_names = ("lat",) if l == nlayers - 1 else ("ctx", "lat")
        for nm in hy_names:
            H = HY[nm]
            L, nt = H["L"], H["nt"]
            P.push()
            zT = P.sb([33, L], F32, "hzT")
            dma(zT, H["z"], [], ["hzT"])
            h1T = P.sb([64, L], F32, "h1T")
            h2T = P.sb([64, L], F32, "h2T")
            nn = min(512, L)
            for t0 in range(0, L, nn):
                ps, pk = P.pbank()
                mm(ps[0:64, :nn], w1s, zT[:, t0:t0 + nn], True, True, ["hw1", "hzT"], [pk])
                sin_mod(h1T[:, t0:t0 + nn], "h1T", ps[0:64, :nn], [pk], 0, nn)
            for t0 in range(0, L, nn):
                ps, pk = P.pbank()
                mm(ps[0:64, :nn], w2s, h1T[:, t0:t0 + nn], True, True, ["hw2", "h1T"], [pk])
                sin_mod(h2T[:, t0:t0 + nn], "h2T", ps[0:64, :nn], [pk], 1, nn)
            hs = P.sb([128, nt, 512], BF16, "hs")
            hd = P.sb([128, nt, 512], BF16, "hd")
            win_rot = SRot(P, [128, 2, 512], F32, 2, "hwin")
            slab_rot = SRot(P, [128, nt, 128], BF16, 4, "hfslab")
            ko_rot = SRot(P, [128, 512], F32, 3, "hko")
            for o in range(2):
                for tt_i in range(nt):
                    wn, wnk = win_rot.next()
                    for dd in range(2):
                        dma(wn[:, dd, :], H["win"][dd, :, tt_i, :], [], [wnk])
                    k0, k0k = t_rot.next()
                    k1, k1k = t_rot.next()
                    for dd, (kk, kkk) in enumerate(((k0, k0k), (k1, k1k))):
                        ps, pk = P.pbank()
                        col0 = (o * 2 + dd) * 512
                        mm(ps, h2T[:, tt_i * 128:(tt_i + 1) * 128], w3s[:, col0:col0 + 512], True, True, ["h2T", "hw3"], [pk])
                        tt(kk, ps, wn[:, dd, :], ALU.mult, [pk, wnk], [kkk])
                    tt(hs[:, tt_i, :], k0, k1, ALU.add, [k0k, k1k], [("hs", tt_i)], eng="dve")
                    tt(hd[:, tt_i, :], k1, k0, ALU.subtract, [k0k, k1k], [("hd", tt_i)], eng="dve")
                for fc in range(nt):
                    for ri, src, sname in ((0, hs, "hs"), (1, hd, "hd")):
                        slab, slk = slab_rot.next()
                        dma(slab, H["fw"][ri, fc], [], [slk])
                        ps, pk = P.pbank()
                        for tt_i in range(nt):
                            mm(ps, slab[:, tt_i, :], src[:, tt_i, :], tt_i == 0, tt_i == nt - 1, [slk, (sname, tt_i)], [pk])
                        ko, kok = ko_rot.next()
                        cp(ko, ps, [pk], [kok], eng="act")
                        dma(H["K"][o, ri, fc], ko, [kok], [("hK", nm, o, ri, fc)])
            P.pop()
        P.pop()

        P.push()
        vz = P.sb([128, 18, 512], BF16, "vz")
        x12 = P.sb([128, 18, 512], BF16, "x12")
        hbias = P.sb([128, 2, 512], F32, "hbias")
        dma(hbias, d_hybias[:, l, :, :], [], ["hbias"])

        def to_token_major(dst, dname, j0):
            P.push()
            chs = [P.sb([128, T], F32, "hch") for j in range(4)]
            pieces = [(0, 384), (384, 1024), (1024, 1664), (1664, T)]
            for (a_, b_) in pieces:
                for j in range(4):
                    dma(chs[j][:, a_:b_], s_hy[j0 + j][:, a_:b_], [("scr", str(s_hy[j0 + j]))], [("hch", j, a_)])
            piece_of = lambda i: [a_ for (a_, b_) in pieces if a_ <= i * 128 < b_][0]
            for i in range(18):
                ps, pk = P.pbank()
                for j in range(4):
                    P.tr(ps[:, j * 128:(j + 1) * 128], chs[j][:, i * 128:(i + 1) * 128], ident, [("hch", j, piece_of(i)), "cst"], [pk])
                cp(dst[:, i, :], ps, [pk], [(dname, i)], eng=("act" if i % 2 else "dve"))
            P.pop()

        def long_conv(o, last):
            for nm in hy_names:
                H = HY[nm]
                nt, tile0 = H["nt"], H["tile0"]
                P.push()
                Y = P.sb([128, 2 * nt, 512], BF16, "hY")
                P.push()
                slab_rot = SRot(P, [128, nt, 128], BF16, 4, "hcslab")
                kk_rot = SRot(P, [128, 2, 512], F32, 2, "hkk")
                tm_rot = SRot(P, [128, 512], F32, 4, "htm")
                for fc in range(nt):
                    pss = []
                    for ri in range(2):
                        slab, slk = slab_rot.next()
                        dma(slab, H["fw"][ri, fc], [], [slk])
                        ps, pk = P.pbank()
                        for tt_i in range(nt):
                            mm(ps, slab[:, tt_i, :], vz[:, tile0 + tt_i, :], tt_i == 0, tt_i == nt - 1, [slk, ("vz", tile0 + tt_i)], [pk])
                        pss.append((ps, pk))
                    kk, kkk = kk_rot.next()
                    for ri in range(2):
                        dma(kk[:, ri, :], H["K"][o, ri, fc], [("hK", nm, o, ri, fc)], [kkk])
                    (pre, prk), (pim, pik) = pss
                    t1, t1k = tm_rot.next()
                    t2, t2k = tm_rot.next()
                    tt(t1, pre, kk[:, 0, :], ALU.mult, [prk, kkk], [t1k])
                    tt(t2, pim, kk[:, 1, :], ALU.mult, [pik, kkk], [t2k])
                    tt(Y[:, fc, :], t1, t2, ALU.add, [t1k, t2k], [("hY", fc)], eng="dve")
                    t3, t3k = tm_rot.next()
                    t4, t4k = tm_rot.next()
                    tt(t3, pim, kk[:, 0, :], ALU.mult, [pik, kkk], [t3k])
                    tt(t4, pre, kk[:, 1, :], ALU.mult, [prk, kkk], [t4k])
                    tt(Y[:, nt + fc, :], t3, t4, ALU.subtract, [t3k, t4k], [("hY", nt + fc)], eng="dve")
                P.pop()
                P.push()
                iv_rot = SRot(P, [128, 2 * nt, 128], BF16, 2, "hiv")
                ta_rot = SRot(P, [128, 512], F32, 3, "hta")
                yh_rot = SRot(P, [128, 512], BF16, 2, "hyh")
                yo_rot = SRot(P, [128, 4, 128], BF16, 2, "hyo")
                for ti in range(nt):
                    iv, ivk = iv_rot.next()
                    dma(iv, H["iv"][ti], [], [ivk])
                    ps, pk = P.pbank()
                    for k in range(2 * nt):
                        mm(ps, iv[:, k, :], Y[:, k, :], k == 0, k == 2 * nt - 1, [ivk, ("hY", k)], [pk])
                    i = tile0 + ti
                    ta, tak = ta_rot.next()
                    tt(ta, vz[:, i, :], hbias[:, o, :], ALU.mult, [("vz", i), "hbias"], [tak], eng="dve")
                    tt(ta, ta, ps, ALU.add, [tak, pk], [tak])
                    if not last:
                        tt(vz[:, i, :], ta, x12[:, i, :], ALU.mult, [tak, ("x12", i)], [("vz", i)])
                    else:
                        yh, yhk = yh_rot.next()
                        tt(yh, ta, x12[:, i, :], ALU.mult, [tak, ("x12", i)], [yhk])
                        psb, pbk = P.pbank(bf=True)
                        for j in range(4):
                            P.tr(psb[:, j * 128:(j + 1) * 128], yh[:, j * 128:(j + 1) * 128], identb, [yhk, "cstb"], [pbk])
                        yo, yok = yo_rot.next()
                        cp(yo.rearrange("p a b -> p (a b)"), psb[:, 0:512], [pbk], [yok], eng="act")
                        for j in range(4):
                            dma(s_y[j][:, i * 128:(i + 1) * 128], yo[:, j, :], [yok], [("scr_y", j, i)])
                P.pop()
                P.pop()

        to_token_major(vz, "vz", 0)
        to_token_major(x12, "x12", 4)
        long_conv(0, False)
        to_token_major(x12, "x12", 8)
        long_conv(1, True)
        P.pop()

    MIXER_IMPL["hy"] = hyena_impl

    def gdn_impl(l):
        PS = P.psum
        PSB = P.psum_bf
        PK = [f"psum{i}" for i in range(8)]
        NCH = 36
        P.push()
        qT = P.sb([128, 4, T], BF16, "gqT")
        kT = P.sb([128, 4, T], BF16, "gkT")
        vT = P.sb([128, 4, T], BF16, "gvT")
        for h in range(4):
            dma(qT[:, h, :], s_gq[h], [("scr", str(s_gq[h]))], ["gqT"])
            dma(kT[:, h, :], s_gk[h], [("scr", str(s_gk[h]))], ["gkT"])
            dma(vT[:, h, :], s_gv[h], [("scr", str(s_gv[h]))], ["gvT"])
        gt = P.sb([64, NCH, 8], F32, "g_tok")
        beta = P.sb([64, NCH, 8], F32, "g_beta")
        gc = P.sb([64, NCH, 8], F32, "g_gc")
        ngc = P.sb([64, NCH, 8], F32, "g_ngc")
        egc = P.sb([64, NCH, 8], F32, "g_egc")
        bexp = P.sb([64, NCH, 8], F32, "g_bexp")
        kdw = P.sb([64, NCH, 8], F32, "g_kdw")
        gtot = P.sb([128, NCH, 8], F32, "g_gtot")
        etot = P.sb([128, NCH, 8], F32, "g_etot")
        fl = lambda t_: t_.rearrange("p a b -> p (a b)")
        P.push()
        abr = P.sb([16, T], F32, "abr")
        dma(abr, s_small[16:32, :], [("scr", str(s_small))], ["abr"])
        for c in range(NCH):
            P.tr(PS[7][0:64, 0:16], abr[0:16, c * 64:(c + 1) * 64], ident[0:16, 0:16], ["abr", "cst"], [PK[7]])
            tt(gt[:, c, :], PS[7][0:64, 0:8], gddtb[0:64, l, :], ALU.add, [PK[7], "gddtb"], ["g_tok"])
            cp(beta[:, c, :], PS[7][0:64, 8:16], [PK[7]], ["g_beta"])
        P.pop()
        act(fl(gt), fl(gt), AF.Exp, ["g_tok"], ["g_tok"])
        act(fl(gt), fl(gt), AF.Ln, ["g_tok"], ["g_tok"], bias=1.0)
        act(fl(beta), fl(beta), AF.Sigmoid, ["g_beta"], ["g_beta"])
        for c in range(NCH):
            tt(gt[:, c, :], gt[:, c, :], gdA[0:64, l, :], ALU.mult, ["g_tok", "gdA"], ["g_tok"])
        for c in range(NCH):
            mm(PS[7][0:64, 0:4], cst[0:64, 2, 0:64], gt[:, c, 0:4], True, True, ["cst", "g_tok"], [PK[7]])
            mm(PS[7][0:64, 4:8], cst[0:64, 3, 0:64], gt[:, c, 4:8], True, True, ["cst", "g_tok"], [PK[7]])
            mm(PS[7][:, 8:16], cst[0:64, 1, :], gt[:, c, :], True, True, ["cst", "g_tok"], [PK[7]])
            cp(gc[:, c, :], PS[7][0:64, 0:8], [PK[7]], ["g_gc"])
            cp(gtot[:, c, :], PS[7][:, 8:16], [PK[7]], ["g_gtot"])
        ts(fl(ngc), fl(gc), -1.0, None, ALU.mult, None, ["g_gc"], ["g_ngc"])
        act(fl(egc), fl(gc), AF.Exp, ["g_gc"], ["g_egc"])
        tt(fl(bexp), fl(egc), fl(beta), ALU.mult, ["g_egc", "g_beta"], ["g_bexp"])
        tt(fl(kdw), fl(gtot[0:64]), fl(gc), ALU.subtract, ["g_gtot", "g_gc"], ["g_kdw"])
        act(fl(kdw), fl(kdw), AF.Exp, ["g_kdw"], ["g_kdw"])
        act(fl(etot), fl(gtot), AF.Exp, ["g_gtot"], ["g_etot"])

        S = P.sb([128, 2, 4, 128], F32, "gS")
        Sb = P.sb([128, 2, 4, 128], BF16, "gSb")
        for d_ in range(2):
            P.memset(S[:, d_], 0.0, [("gS", d_)])
            P.memset(Sb[:, d_], 0.0, [("gSb", d_)])

        def bcn(ap8, d, n, np_=64):
            return bass.AP(ap8.tensor, ap8.offset + d * 4, [list(ap8.ap[0]), [1, 4], [0, n]])

        f4 = lambda t_: t_.rearrange("p a b -> p (a b)")
        v4 = lambda ap_: ap_.rearrange("p (h k) -> p h k", h=4)
        NSLOT = 2
        prep_n = [0, 0]
        rec_n = [0, 0]
        pools = []
        for d_ in range(2):
            nm = f"g{d_}"
            pools.append(dict(
                kbe=SRot(P, [64, 4, 128], BF16, 1, nm + "kbe"), vb=SRot(P, [64, 4, 128], BF16, 1, nm + "vb"),
                kd=SRot(P, [64, 4, 128], BF16, NSLOT, nm + "kd"), vn=SRot(P, [64, 4, 128], BF16, 1, nm + "vn"),
                sc=SRot(P, [64, 4, 64], F32, 2, nm + "sc"), qkm=SRot(P, [64, 4, 64], BF16, NSLOT, nm + "qkm"),
                X=SRot(P, [64, 4, 64], F32, 2, nm + "X"), Y=SRot(P, [64, 4, 64], F32, 2, nm + "Y"),
                Qf=SRot(P, [64, 4, 64], F32, 2, nm + "Qf"), Qb=SRot(P, [64, 4, 64], BF16, 1, nm + "Qb"),
                wT=SRot(P, [128, 4, 64], BF16, NSLOT, nm + "wT"), u=SRot(P, [64, 4, 128], F32, NSLOT, nm + "u"),
                tA=SRot(P, [64, 4, 128], F32, 1, nm + "tA")))
        handoff = [dict(), dict()]

        def prep(d, order):
            pl = pools[d]
            B0, B1 = 4 * d, 4 * d + 1
            tri = cst[0:64, 2 if d == 0 else 3, 0:64]
            mA = gmask[:, d, :, :]
            mQ = gmask[:, 2 + d, :, :]
            for n_, c in enumerate(order):
                while rec_n[d] < n_ - NSLOT + 1:
                    yield
                csl = slice(c * 64, (c + 1) * 64)
                for h in range(4):
                    P.tr(PSB[B0][0:64, h * 128:(h + 1) * 128], kT[:, h, csl], identb, ["gkT", "cstb"], [PK[B0]])
                for h in range(4):
                    mm(PS[B1][0:64, h * 64:(h + 1) * 64], kT[:, h, csl], kT[:, h, csl], True, True, ["gkT"], [PK[B1]])
                    mm(PS[B1][0:64, 256 + h * 64:256 + (h + 1) * 64], kT[:, h, csl], qT[:, h, csl], True, True, ["gkT", "gqT"], [PK[B1]])
                kbe, kbek = pl["kbe"].next()
                kd, kdk = pl["kd"].next()
                tt(kbe, v4(PSB[B0][0:64, 0:512]), bcn(bexp[:, c, :], d, 128), ALU.mult, [PK[B0], "g_bexp"], [kbek])
                tt(kd, v4(PSB[B0][0:64, 0:512]), bcn(kdw[:, c, :], d, 128), ALU.mult, [PK[B0], "g_kdw"], [kdk])
                yield
                for h in range(4):
                    P.tr(PSB[B0][0:64, h * 128:(h + 1) * 128], vT[:, h, csl], identb, ["gvT", "cstb"], [PK[B0]])
                vb, vbk = pl["vb"].next()
                tt(vb, v4(PSB[B0][0:64, 0:512]), bcn(beta[:, c, :], d, 128), ALU.mult, [PK[B0], "g_beta"], [vbk])
                gU, gUk = pl["sc"].next()
                for h in range(4):
                    act(gU[:, h, :], tri, AF.Copy, ["cst", "g_tok"], [gUk], scale=gt[:, c, d * 4 + h:d * 4 + h + 1])
                yield
                mm(PS[B0][0:64, 0:256], cst[0:64, 1, 0:64], f4(gU), True, True, ["cst", gUk], [PK[B0]])
                R3 = PS[B0][0:64, 0:256].rearrange("p (h k) -> p h k", h=4)
                e, ek = pl["sc"].next()
                tt(e, R3, bcn(ngc[:, c, :], d, 64), ALU.add, [PK[B0], "g_ngc"], [ek])
                ts(f4(e), f4(e), 0.0, None, ALU.min, None, [ek], [ek])
                yield
                act(f4(e), f4(e), AF.Exp, [ek], [ek])
                tt(e, e, mQ, ALU.mult, [ek, "gmask"], [ek], eng="dve")
                yield
                qkm, qkmk = pl["qkm"].next()
                tt(f4(qkm), PS[B1][0:64, 256:512], f4(e), ALU.mult, [PK[B1], ek], [qkmk])
                e2, e2k = pl["sc"].next()
                tt(e2, R3, bcn(ngc[:, c, :], d, 64), ALU.add, [PK[B0], "g_ngc"], [e2k])
                ts(f4(e2), f4(e2), 0.0, None, ALU.max, None, [e2k], [e2k])
                yield
                act(f4(e2), f4(e2), AF.Exp, [e2k], [e2k], scale=-1.0)
                tt(e2, e2, mA, ALU.mult, [e2k, "gmask"], [e2k], eng="dve")
                tt(e2, e2, bcn(beta[:, c, :], d, 64), ALU.mult, [e2k, "g_beta"], [e2k], eng="dve")
                yield
                X, Xk = pl["X"].next()
                tt(f4(X), PS[B1][0:64, 0:256], f4(e2), ALU.mult, [PK[B1], e2k], [Xk])
                yield
                for h in range(4):
                    P.tr(PS[B0][0:64, 256 + h * 64:256 + (h + 1) * 64], X[:, h, :], ident[0:64, 0:64], [Xk, "cst"], [PK[B0]])
                Y, Yk = pl["Y"].next()
                cp(f4(Y), PS[B0][0:64, 256:512], [PK[B0]], [Yk], eng="act")
                yield
                Qf, Qfk = pl["Qf"].next()
                tt(f4(Qf), f4(gI4), f4(Y), ALU.subtract, ["gI4", Yk], [Qfk])
                for lev in range(5):
                    Xn, Xnk = pl["X"].next()
                    for h in range(4):
                        mm(PS[B1][0:64, h * 64:(h + 1) * 64], Y[:, h, :], X[:, h, :], True, True, [Yk, Xk], [PK[B1]])
                    if lev < 4:
                        Yn, Ynk = pl["Y"].next()
                        for h in range(4):
                            mm(PS[B1][0:64, 256 + h * 64:256 + (h + 1) * 64], X[:, h, :], Y[:, h, :], True, True, [Yk, Xk], [PK[B1]])
                    yield
                    cp(f4(Xn), PS[B1][0:64, 0:256], [PK[B1]], [Xnk], eng="act")
                    if lev < 4:
                        cp(f4(Yn), PS[B1][0:64, 256:512], [PK[B1]], [Ynk], eng="act")
                    yield
                    for h in range(4):
                        mm(PS[B0][0:64, h * 64:(h + 1) * 64], Xn[:, h, :], Qf[:, h, :], True, True, [Xnk, Qfk], [PK[B0]])
                    yield
                    Qfn, Qfnk = pl["Qf"].next()
                    tt(f4(Qfn), f4(Qf), PS[B0][0:64, 0:256], ALU.add, [Qfk, PK[B0]], [Qfnk])
                    X, Xk = Xn, Xnk
                    if lev < 4:
                        Y, Yk = Yn, Ynk
                    Qf, Qfk = Qfn, Qfnk
                    yield
                Qb, Qbk = pl["Qb"].next()
                cp(f4(Qb), f4(Qf), [Qfk], [Qbk], eng="act")
                yield
                for h in range(4):
                    mm(PS[B1][0:64, h * 128:(h + 1) * 128], Qb[:, h, :], vb[:, h, :], True, True, [Qbk, vbk], [PK[B1]])
                    mm(PS[B0][:, 256 + h * 64:256 + (h + 1) * 64], kbe[:, h, :], Qb[:, h, :], True, True, [Qbk, kbek], [PK[B0]])
                yield
                u, uk = pl["u"].next()
                cp(f4(u), PS[B1][0:64, 0:512], [PK[B1]], [uk], eng="act")
                wT, wTk = pl["wT"].next()
                cp(f4(wT), PS[B0][:, 256:512], [PK[B0]], [wTk])
                handoff[d][c] = (kd, kdk, qkm, qkmk, u, uk, wT, wTk)
                prep_n[d] = n_ + 1
                yield

        def rec(d, order):
            pl = pools[d]
            R0, R1 = 4 * d + 2, 4 * d + 3
            Sd, Sbd = S[:, d], Sb[:, d]
            for n_, c in enumerate(order):
                while prep_n[d] < n_ + 1:
                    yield
                kd, kdk, qkm, qkmk, u, uk, wT, wTk = handoff[d].pop(c)
                csl = slice(c * 64, (c + 1) * 64)
                for h in range(4):
                    mm(PS[R0][0:64, h * 128:(h + 1) * 128], wT[:, h, :], Sbd[:, h, :], True, True, [wTk, ("gSb", d)], [PK[R0]])
                    mm(PS[R1][0:64, h * 128:(h + 1) * 128], qT[:, h, csl], Sbd[:, h, :], True, True, ["gqT", ("gSb", d)], [PK[R1]])
                yield
                vn, vnk = pl["vn"].next()
                tt(f4(vn), f4(u), PS[R0][0:64, 0:512], ALU.subtract, [uk, PK[R0]], [vnk])
                tA, tAk = pl["tA"].next()
                tt(tA, v4(PS[R1][0:64, 0:512]), bcn(egc[:, c, :], d, 128), ALU.mult, [PK[R1], "g_egc"], [tAk])
                yield
                for h in range(4):
                    mm(PS[R0][:, h * 128:(h + 1) * 128], kd[:, h, :], vn[:, h, :], True, True, [kdk, vnk], [PK[R0]])
                    mm(PS[R1][0:64, h * 128:(h + 1) * 128], qkm[:, h, :], vn[:, h, :], True, True, [qkmk, vnk], [PK[R1]])
                yield
                tt(Sd, Sd, bass.AP(etot.tensor, etot[:, c, :].offset + d * 4, [list(etot[:, c, :].ap[0]), [1, 4], [0, 128]]), ALU.mult,
                   [("gS", d), "g_etot"], [("gS", d)])
                tt(f4(Sd), f4(Sd), PS[R0][:, 0:512], ALU.add, [("gS", d), PK[R0]], [("gS", d)])
                yield
                cp(f4(Sbd), f4(Sd), [("gS", d)], [("gSb", d)], eng="act")
                tt(f4(tA), f4(tA), PS[R1][0:64, 0:512], ALU.add, [tAk, PK[R1]], [tAk])
                dma(s_go[d, c], f4(tA), [tAk], [("scr_go", d, c)])
                rec_n[d] = n_ + 1
                yield

        orders = [list(range(NCH)), [3, 2, 1, 0] + list(range(NCH - 1, 3, -1))]
        gens = [prep(0, orders[0]), prep(1, orders[1]), rec(0, orders[0]), rec(1, orders[1])]
        while gens:
            for g_ in list(gens):
                try:
                    next(g_)
                except StopIteration:
                    gens.remove(g_)
        P.pop()

        P.push()
        of_r = SRot(P, [64, 2, 512], F32, 2, "gof")
        sq_r = SRot(P, [64, 4, 128], F32, 2, "gsq")
        ss_r = SRot(P, [64, 4], F32, 2, "gss")
        on_r = SRot(P, [64, 4, 128], BF16, 2, "gon")
        gg_r = SRot(P, [128, 4, 64], BF16, 2, "ggc")
        yo_r = SRot(P, [128, 4, 64], BF16, 2, "gyo")
        gg_keys = [("scr", str(s_gg[h])) for h in range(4)]
        gg_pht = s_gg.rearrange("h p t -> p h t")
        for c in range(NCH):
            csl = slice(c * 64, (c + 1) * 64)
            of, ofk = of_r.next()
            for d_ in range(2):
                dma(of[:, d_, :], s_go[d_, c], [("scr_go", d_, c)], [ofk])
            o = of[:, 0, :]
            tt(o, o, of[:, 1, :], ALU.add, [ofk], [ofk], eng="dve")
            sq, sqk = sq_r.next()
            tt(f4(sq), o, o, ALU.mult, [ofk], [sqk], eng="dve")
            ss, ssk = ss_r.next()
            P.emit("dve", lambda e_, o_=ss, i_=sq: e_.tensor_reduce(out=o_, in_=i_, axis=AX.X, op=ALU.add), [sqk], [ssk])
            ts(ss, ss, 1.0 / 128, EPS, ALU.mult, ALU.add, [ssk], [ssk])
            act(ss, ss, AF.Sqrt, [ssk], [ssk])
            P.emit("dve", lambda e_, o_=ss: e_.reciprocal(o_, o_), [ssk], [ssk])
            on, onk = on_r.next()
            tt(on, v4(o), bass.AP(ss.tensor, ss.offset, [list(ss.ap[0]), [1, 4], [0, 128]]), ALU.mult, [ofk, ssk], [onk])
            psb, pbk = P.pbank(bf=True)
            for h in range(4):
                P.tr(psb[:, h * 64:(h + 1) * 64], on[:, h, :], identb[0:64, 0:64], [onk, "cstb"], [pbk])
            gg, ggk = gg_r.next()
            dma(gg, gg_pht[:, :, csl], gg_keys, [ggk])
            yo, yok = yo_r.next()
            stt(f4(yo), psb[:, 0:256], gdnw[:, l:l + 1], f4(gg), ALU.mult, ALU.mult, [pbk, "gdnw", ggk], [yok])
            for h in range(4):
                dma(s_y[8 + h][:, csl], yo[:, h, :], [yok], [("scr_y", 8 + h, c)], q="pool")
        P.pop()

    MIXER_IMPL["gdn"] = gdn_impl

    def ssd_impl(l):
        Uc, Loc = cst[:, 2, :], cst[:, 3, :]
        PS = P.psum
        PK = [f"psum{i}" for i in range(8)]
        NTI = 18
        P.push()
        ytok = P.sb([128, NTI, 512], F32, "ytok")
        sx_keys = [("scr", str(s_sx[j])) for j in range(4)]
        sx_pjt = s_sx.rearrange("j p t -> p j t")
        P.push()
        BT = P.sb([128, 2, T], BF16, "BT")
        CT = P.sb([128, 2, T], BF16, "CT")
        Btok = P.sb([128, NTI, 256], BF16, "Btok")
        for g in range(2):
            dma(BT[:, g, :], s_sbc[g], [("scr", str(s_sbc[g]))], ["BT"])
            dma(CT[:, g, :], s_sbc[2 + g], [("scr", str(s_sbc[2 + g]))], ["CT"])
        dt = P.sb([128, NTI, 16], F32, "dt")
        av = P.sb([128, NTI, 16], F32, "av")
        acum = P.sb([128, NTI, 16], F32, "acum")
        nacum = P.sb([128, NTI, 16], F32, "nacum")
        eacum = P.sb([128, NTI, 16], F32, "eacum")
        atot = P.sb([128, NTI, 16], F32, "atot")
        etot = P.sb([128, NTI, 16], F32, "etot")
        dtw = P.sb([128, NTI, 16], F32, "dtw")
        P.push()
        dtr = P.sb([16, T], F32, "dtr")
        dma(dtr, s_small[0:16, :], [("scr", str(s_small))], ["dtr"])
        for i in range(NTI):
            ps, pk = PS[7], PK[7]
            P.tr(ps[:, 0:16], dtr[0:16, i * 128:(i + 1) * 128], ident[0:16, 0:16], ["dtr", "cst"], [pk])
            tt(dt[:, i, :], ps[:, 0:16], ssdtb[:, l, :], ALU.add, [pk, "ssdtb"], ["dt"])
        P.pop()
        dtf = dt.rearrange("p a b -> p (a b)")
        act(dtf, dtf, AF.Exp, ["dt"], ["dt"])
        act(dtf, dtf, AF.Ln, ["dt"], ["dt"], bias=1.0)
        for i in range(NTI):
            tt(av[:, i, :], dt[:, i, :], ssA[:, l, :], ALU.mult, ["dt", "ssA"], ["av"])
        for i in range(NTI):
            ps, pk = PS[7], PK[7]
            mm(ps[:, 0:8], Uc, av[:, i, 0:8], True, True, ["cst", "av"], [pk])
            mm(ps[:, 8:16], Loc, av[:, i, 8:16], True, True, ["cst", "av"], [pk])
            mm(ps[:, 16:32], ones, av[:, i, :], True, True, ["cst", "av"], [pk])
            cp(acum[:, i, :], ps[:, 0:16], [pk], ["acum"])
            cp(atot[:, i, :], ps[:, 16:32], [pk], ["atot"])
        fl = lambda t_: t_.rearrange("p a b -> p (a b)")
        ts(fl(nacum), fl(acum), -1.0, None, ALU.mult, None, ["acum"], ["nacum"])
        act(fl(eacum), fl(acum), AF.Exp, ["acum"], ["eacum"])
        act(fl(etot), fl(atot), AF.Exp, ["atot"], ["etot"])
        tt(fl(dtw), fl(atot), fl(acum), ALU.subtract, ["atot", "acum"], ["dtw"])
        act(fl(dtw), fl(dtw), AF.Exp, ["dtw"], ["dtw"])
        tt(fl(dtw), fl(dtw), fl(dt), ALU.mult, ["dtw", "dt"], ["dtw"])
        for i in range(NTI):
            psb, pk = P.psum_bf[7], PK[7]
            for g in range(2):
                P.tr(psb[:, g * 128:(g + 1) * 128], BT[:, g, i * 128:(i + 1) * 128], identb, ["BT", "cstb"], [pk])
            cp(Btok[:, i, :], psb[:, 0:256], [pk], ["Btok"], eng="act")

        hin = P.sb([128, 2, 512], F32, "hin")
        hinb = P.sb([128, 2, 512], BF16, "hinb")
        for d_ in range(2):
            P.memset(hin[:, d_, :], 0.0, [("hin", d_)])
            P.memset(hinb[:, d_, :], 0.0, [("hinb", d_)])
        P.memset(ytok.rearrange("p a b -> p (a b)"), 0.0, ["ytok0"])
        for i in range(NTI):
            P.buf[("ytok", i)] = [P.buf["ytok0"][0], {}, []]

        def bc64(ap16, d):
            return bass.AP(ap16.tensor, ap16.offset + d * 8, [list(ap16.ap[0]), [1, 8], [0, 64]])

        v3 = lambda t_: t_.rearrange("p (h o) -> p h o", h=8)

        def sweep(d, order):
            nm = f"s{d}"
            tri = Uc if d == 0 else Loc
            B0, B1, B2, B3 = 4 * d, 4 * d + 1, 4 * d + 2, 4 * d + 3
            aU = P.sb([128, 4, 128], F32, nm + "aU")
            xd_rot = SRot(P, [128, 512], BF16, 2, nm + "xd")
            xdw_rot = SRot(P, [128, 512], BF16, 2, nm + "xdw")
            cbm_rot = SRot(P, [128, 2, 128], F32, 2, nm + "cbm")
            dm_rot = SRot(P, [128, 128], F32, 2, nm + "dmin")
            e_rot = SRot(P, [128, 128], F32, 2, nm + "edec")
            mt_rot = SRot(P, [128, 128], BF16, 2, nm + "mt")
            t5_rot = SRot(P, [128, 512], F32, 2, nm + "t5")
            xs_rot = SRot(P, [128, 4, 128], F32, 2, nm + "xs_t")
            yield
            for i in order:
                tsl = slice(i * 128, (i + 1) * 128)
                xs_, xsk = xs_rot.next()
                dma(xs_, sx_pjt[:, :, tsl], sx_keys, [xsk])
                for j in range(4):
                    P.tr(PS[B0][:, j * 128:(j + 1) * 128], xs_[:, j, :], ident, [xsk, "cst"], [PK[B0]])
                for g in range(2):
                    mm(PS[B3][:, g * 256:(g + 1) * 256], CT[:, g, tsl], hinb[:, d, g * 256:(g + 1) * 256], True, True,
                       ["CT", ("hinb", d)], [PK[B3]])
                yield
                xd, xdk = xd_rot.next()
                xdw, xdwk = xdw_rot.next()
                tt(v3(xd), v3(PS[B0][:, 0:512]), bc64(dt[:, i, :], d), ALU.mult, [PK[B0], "dt"], [xdk])
                tt(v3(xdw), v3(PS[B0][:, 0:512]), bc64(dtw[:, i, :], d), ALU.mult, [PK[B0], "dtw"], [xdwk])
                yield
                cbm, cbk = cbm_rot.next()
                for g in range(2):
                    mm(PS[B0][:, g * 128:(g + 1) * 128], BT[:, g, tsl], CT[:, g, tsl], True, True, ["BT", "CT"], [PK[B0]])
                yield
                for g in range(2):
                    tt(cbm[:, g, :], PS[B0][:, g * 128:(g + 1) * 128], tri, ALU.mult, [PK[B0], "cst"], [cbk])
                t5, t5k = t5_rot.next()
                tt(v3(t5), v3(PS[B3][:, 0:512]), bc64(eacum[:, i, :], d), ALU.mult, [PK[B3], "eacum"], [t5k])
                yield
                for g in range(2):
                    mm(PS[B0][:, g * 256:(g + 1) * 256], Btok[:, i, g * 128:(g + 1) * 128], xdw[:, g * 256:(g + 1) * 256], True, True,
                       ["Btok", xdwk], [PK[B0]])
                for rnd in range(2):
                    for hq in range(4):
                        h = rnd * 4 + hq
                        act(aU[:, hq, :], tri, AF.Copy, ["cst", "av"], [(nm + "aU", hq)], scale=av[:, i, d * 8 + h:d * 8 + h + 1])
                    yield
                    mm(PS[B1][:, 0:512], ones, aU.rearrange("p a b -> p (a b)"), True, True,
                       ["cst"] + [(nm + "aU", q_) for q_ in range(4)], [PK[B1]])
                    yield
                    for hq in range(4):
                        h = rnd * 4 + hq
                        dmin, dmk = dm_rot.next()
                        ts(dmin, PS[B1][:, hq * 128:(hq + 1) * 128], nacum[:, i, d * 8 + h:d * 8 + h + 1], 0.0,
                           ALU.add, ALU.min, [PK[B1], "nacum"], [dmk])
                        yield
                        ed, edk = e_rot.next()
                        act(ed, dmin, AF.Exp, [dmk], [edk])
                        yield
                        mt, mtk = mt_rot.next()
                        tt(mt, cbm[:, h // 4, :], ed, ALU.mult, [cbk, edk], [mtk])
                        yield
                        mm(PS[B2][:, h * 64:(h + 1) * 64], mt, xd[:, h * 64:(h + 1) * 64], True, True, [mtk, xdk], [PK[B2]])
                yield
                tt(t5, PS[B2][:, 0:512], t5, ALU.add, [PK[B2], t5k], [t5k])
                tt(ytok[:, i, :], ytok[:, i, :], t5, ALU.add, [("ytok", i), t5k], [("ytok", i)], eng="dve")
                tt(v3(hin[:, d, :]), v3(hin[:, d, :]), bc64(etot[:, i, :], d), ALU.mult, [("hin", d), "etot"], [("hin", d)])
                yield
                tt(hin[:, d, :], hin[:, d, :], PS[B0][:, 0:512], ALU.add, [("hin", d), PK[B0]], [("hin", d)])
                yield
                cp(hinb[:, d, :], hin[:, d, :], [("hin", d)], [("hinb", d)], eng="act")
                yield

        gens = [sweep(0, list(range(NTI))), sweep(1, [1, 0] + list(range(NTI - 1, 1, -1)))]
        while gens:
            for g_ in list(gens):
                try:
                    next(g_)
                except StopIteration:
                    gens.remove(g_)
        P.pop()
        szr = SRot(P, [128, 4, 512], BF16, 2, "szt")
        y2r = SRot(P, [128, 4, 512], F32, 2, "y2")
        sqr = SRot(P, [128, 512], F32, 2, "ysq")
        rsr = SRot(P, [128, 2, 512], F32, 2, "yrs")
        obr = SRot(P, [128, 4, 512], BF16, 2, "yob")
        xsr = SRot(P, [128, 4, 512], F32, 2, "xsf")
        for (t0, n, w) in TT:
            sz, szk = szr.next()
            for j in range(4):
                dma(sz[:, j, 0:n], s_sz[j][:, t0:t0 + n], [("scr", str(s_sz[j]))], [szk])
            y2, y2k = y2r.next()
            xsT, xsTk = xsr.next()
            dma(xsT[:, :, 0:n], sx_pjt[:, :, t0:t0 + n], sx_keys, [xsTk])
            for j in range(4):
                for q_ in range(n // 128):
                    i = t0 // 128 + q_
                    P.tr(PS[j][:, q_ * 128:(q_ + 1) * 128], ytok[:, i, j * 128:(j + 1) * 128], ident, [("ytok", i), "cst"], [PK[j]])
                stt(y2[:, j, 0:n], xsT[:, j, 0:n], ssDpc[:, l, j:j + 1], PS[j][:, 0:n], ALU.mult, ALU.add,
                    [xsTk, "ssDpc", PK[j]], [y2k])
                tt(y2[:, j, 0:n], y2[:, j, 0:n], sz[:, j, 0:n], ALU.mult, [y2k, szk], [y2k])
            rs, rsk = rsr.next()
            for g in range(2):
                for jj in range(2):
                    sq, sqk = sqr.next()
                    act(sq[:, 0:n], y2[:, 2 * g + jj, 0:n], AF.Square, [y2k], [sqk])
                    mm(PS[4 + g][:, 0:n], ones, sq[:, 0:n], jj == 0, jj == 1, ["cst", sqk], [PK[4 + g]])
                ts(rs[:, g, 0:n], PS[4 + g][:, 0:n], 1.0 / 256, EPS, ALU.mult, ALU.add, [PK[4 + g]], [rsk])
                act(rs[:, g, 0:n], rs[:, g, 0:n], AF.Sqrt, [rsk], [rsk])
                P.emit("dve", lambda e, o=rs[:, g, 0:n]: e.reciprocal(o, o), [rsk], [rsk])
            ob, obk = obr.next()
            for j in range(4):
                stt(ob[:, j, 0:n], y2[:, j, 0:n], ssnw[:, l, j:j + 1], rs[:, j // 2, 0:n], ALU.mult, ALU.mult, [y2k, "ssnw", rsk], [obk])
                dma(s_y[4 + j][:, t0:t0 + n], ob[:, j, 0:n], [obk], [("scr_y", 4 + j, t0)])
        P.pop()

    MIXER_IMPL["ssm"] = ssd_impl
    hyena = MIXER_IMPL.get("hy", hyena)
    ssd = MIXER_IMPL.get("ssm", ssd)
    gdn = MIXER_IMPL.get("gdn", gdn)

    for l in range(nlayers):
        layer(l)

    P.push()
    oT = P.sb([128, 8, NLAT], F32, "oT")
    xl = None
    norm_mod_out = oT

    class _Shift:
        pass
    lat_tiles = [(t0, n, 0) for (t0, n, w) in TT if w == 0]
    sq_rot = SRot(P, [128, 512], BF16, 3, "fsq")
    rstd_rot = SRot(P, [128, 512], F32, 2, "frstd")
    for (t0, n, w) in lat_tiles:
        ps, pk = P.pbank()
        for c in range(8):
            sq, sqk = sq_rot.next()
            act(sq[:, :n], xT[:, c, t0:t0 + n], AF.Square, [("xT", c)], [sqk])
            mm(ps[:, :n], onesb, sq[:, :n], c == 0, c == 7, ["cstb", sqk], [pk])
        rstd, rk = rstd_rot.next()
        ts(rstd[:, :n], ps[:, :n], 1.0 / D, EPS, ALU.mult, ALU.add, [pk], [rk])
        act(rstd[:, :n], rstd[:, :n], AF.Sqrt, [rk], [rk])
        P.emit("dve", lambda e, o=rstd[:, :n]: e.reciprocal(o, o), [rk], [rk])
        for c in range(8):
            stt(oT[:, c, t0 - NCTX:t0 - NCTX + n], xT[:, c, t0:t0 + n], fnw[:, c:c + 1], rstd[:, :n], ALU.mult, ALU.mult,
                [("xT", c), rk, "fnw"], [("oT", c, t0)])
            dma(d_outT[:, c, t0 - NCTX:t0 - NCTX + n], oT[:, c, t0 - NCTX:t0 - NCTX + n], [("oT", c, t0)], [("OUT", c, t0)], is_out=True)
    P.pop()
    P.finish()
    P.build()
    return nc, list(dbg_out.keys())


def kernel(**inputs):
    nc, _ = build_program()
    sh = prep_shared(inputs)
    in_maps = [dict(sh, **prep_core(inputs, b)) for b in range(8)]
    res = run_bass_kernel_spmd(nc, in_maps, core_ids=list(range(8)))
    out = np.stack([np.asarray(r["outT"], np.float32).transpose(2, 1, 0).reshape(NLAT, D) for r in res.results])
    return np.ascontiguousarray(out.astype(np.float32))
```

```python
import contextlib
import numpy as np
import ml_dtypes
import concourse.bass as bass
import concourse.mybir as mybir
from concourse.bass_utils import run_bass_kernel_spmd

F32 = mybir.dt.float32
BF16 = mybir.dt.bfloat16
AF = mybir.ActivationFunctionType
ALU = mybir.AluOpType
AX = mybir.AxisListType

COMPUTE = ("pe", "act", "dve", "pool")
DMAQ = ("sp", "pool", "act")
NEPOCH = 10
EPOCH_MAX = 24000
DK = 6


class Prog:
    def __init__(self, nc):
        self.nc = nc
        self.es = contextlib.ExitStack()
        self.ops = {e: [] for e in ("pe", "act", "dve", "pool", "sp")}
        self.cnt = {e: 0 for e in COMPUTE}
        self.epoch = {e: 0 for e in COMPUTE}
        self.csem = {e: [self.es.enter_context(nc.semaphore(f"s_{e}_{i}")) for i in range(NEPOCH)] for e in COMPUTE}
        self.dsem = {q: [self.es.enter_context(nc.semaphore(f"d_{q}_{i}")) for i in range(DK)] for q in DMAQ}
        self.dcnt = {q: 0 for q in DMAQ}
        self.known_c = {e: {} for e in self.ops}
        self.known_d = {e: set() for e in self.ops}
        self.buf = {}
        self.n_alloc = 0
        self.n_inst = 0
        self.out_events = []

    def sb(self, shape, dtype, name=None):
        self.n_alloc += 1
        name = name or f"t{self.n_alloc}"
        return self.es.enter_context(self.nc.sbuf_tensor(f"{name}_{self.n_alloc}", list(shape), dtype))

    def ps(self, shape, dtype, name=None):
        self.n_alloc += 1
        name = name or f"p{self.n_alloc}"
        return self.es.enter_context(self.nc.psum_tensor(f"{name}_{self.n_alloc}", list(shape), dtype))

    def _wait(self, eng, ev):
        if ev[0] == "c":
            _, e2, ep, c = ev
            if eng == "pe" and e2 == "pe":
                return
            k = self.known_c[eng]
            if k.get((e2, ep), 0) >= c:
                return
            k[(e2, ep)] = c
            sem = self.csem[e2][ep]
            self.ops[eng].append(lambda e, sem=sem, c=c: e.wait_ge(sem, c))
        else:
            _, q, idx = ev
            if ev in self.known_d[eng]:
                return
            self.known_d[eng].add(ev)
            sem = self.dsem[q][idx % DK]
            val = 16 * (idx // DK + 1)
            self.ops[eng].append(lambda e, sem=sem, val=val: e.wait_ge(sem, val))

    def emit(self, eng, fn, reads=(), writes=(), dma=False):
        deps = []
        for b in reads:
            st = self.buf.get(b)
            if st and st[0] is not None:
                deps.append(st[0])
            if st and isinstance(b, str) and b.startswith("psum"):
                deps.extend(ev_ for e_, ev_ in st[1].items() if e_ != eng)
        for b in writes:
            st = self.buf.get(b)
            if st:
                if st[0] is not None:
                    deps.append(st[0])
                deps.extend(st[1].values())
                deps.extend(st[2])
        for ev in deps:
            self._wait(eng, ev)
        self.n_inst += 1
        if dma:
            q = eng
            idx = self.dcnt[q]
            self.dcnt[q] += 1
            if idx >= DK:
                self._wait(eng, ("d", q, idx - DK))
            sem = self.dsem[q][idx % DK]
            self.ops[eng].append(lambda e, fn=fn, sem=sem: fn(e).then_inc(sem, 16))
            ev = ("d", q, idx)
        else:
            if self.cnt[eng] >= EPOCH_MAX:
                self.epoch[eng] += 1
                self.cnt[eng] = 0
                assert self.epoch[eng] < NEPOCH
            self.cnt[eng] += 1
            ep = self.epoch[eng]
            c = self.cnt[eng]
            sem = self.csem[eng][ep]
            self.ops[eng].append(lambda e, fn=fn, sem=sem: fn(e).then_inc(sem, 1))
            ev = ("c", eng, ep, c)
        for b in writes:
            self.buf[b] = [ev, {}, []]
        for b in reads:
            st = self.buf.setdefault(b, [None, {}, []])
            if ev[0] == "c":
                st[1][ev[1]] = ev
            else:
                st[2].append(ev)
        return ev

    def finish(self):
        for ev in self.out_events:
            self._wait("sp", ev)

    def build(self):
        nc = self.nc
        ops = self.ops
        with nc.Block() as block:
            @block.tensor
            def _(e):
                for f in ops["pe"]:
                    f(e)

            @block.scalar
            def _(e):
                for f in ops["act"]:
                    f(e)

            @block.vector
            def _(e):
                for f in ops["dve"]:
                    f(e)

            @block.gpsimd
            def _(e):
                for f in ops["pool"]:
                    f(e)

            @block.sync
            def _(e):
                for f in ops["sp"]:
                    f(e)
        self.es.close()

    def mm(self, out, lhsT, rhs, start, stop, rd, wr):
        self.emit("pe", lambda e: e.matmul(out, lhsT, rhs, start=start, stop=stop), rd, wr)

    def tr(self, out, in_, ident, rd, wr):
        self.emit("pe", lambda e: e.transpose(out, in_, ident), rd, wr)

    def act(self, out, in_, func, rd, wr, bias=None, scale=None):
        kw = {}
        if bias is not None:
            kw["bias"] = bias
        if scale is not None:
            kw["scale"] = scale
        self.emit("act", lambda e: e.activation(out, in_, func, **kw), rd, wr)

    def tt(self, out, in0, in1, op, rd, wr, eng="dve"):
        self.emit(eng, lambda e: e.tensor_tensor(out, in0, in1, op), rd, wr)

    def ts(self, out, in0, s1, s2, op0, op1, rd, wr, eng="dve"):
        if op1 is None:
            self.emit(eng, lambda e: e.tensor_scalar(out, in0, s1, None, op0), rd, wr)
        else:
            self.emit(eng, lambda e: e.tensor_scalar(out, in0, s1, s2, op0, op1), rd, wr)

    def stt(self, out, in0, scalar, in1, op0, op1, rd, wr, eng="dve"):
        self.emit(eng, lambda e: e.scalar_tensor_tensor(out, in0, scalar, in1, op0, op1), rd, wr)

    def cp(self, out, in_, rd, wr, eng="dve"):
        if eng == "act":
            self.emit("act", lambda e: e.copy(out, in_), rd, wr)
        else:
            self.emit(eng, lambda e: e.tensor_copy(out, in_), rd, wr)

    def memset(self, ap, val, wr, eng="dve"):
        self.emit(eng, lambda e: e.memset(ap, val), (), wr)

    def dma(self, out, in_, rd, wr, q="sp", is_out=False):
        ev = self.emit(q, lambda e: e.dma_start(out=out, in_=in_), rd, wr, dma=True)
        if is_out:
            self.out_events.append(ev)
        return ev


class Rot:
    def __init__(self, P, shape, dtype, n, name, psum=False):
        self.t = [(P.ps if psum else P.sb)(shape, dtype, f"{name}{i}") for i in range(n)]
        self.keys = [f"{name}#{i}" for i in range(n)]
        self.i = 0

    def next(self):
        i = self.i % len(self.t)
        self.i += 1
        return self.t[i], self.keys[i]


import math

D = 1024
T = 2304
NCTX = 256
NLAT = 2048
DFF = 2816
NL = 4
EPS = 1e-6
ADA_INTERLEAVE = False
TT = [(0, 256, 1), (256, 512, 0), (768, 512, 0), (1280, 512, 0), (1792, 512, 0)]
ARENA_BYTES = 207 * 1024


def _prod(s):
    r = 1
    for v in s:
        r *= v
    return r


class KProg(Prog):
    def __init__(self, nc):
        super().__init__(nc)
        self.arena = self.es.enter_context(nc.sbuf_tensor("arena", [128, ARENA_BYTES // 2], BF16))
        self.arena_f = self.arena.bitcast(F32)
        self.sp_ = 0
        self.stack = []
        self.psum = [self.es.enter_context(nc.psum_tensor(f"psb{i}", [128, 512], F32)) for i in range(8)]
        self.psum_bf = [p.bitcast(BF16) for p in self.psum]
        self.ps_i = 0

    def sb(self, shape, dtype, name=None):
        shape = list(shape)
        n = _prod(shape[1:])
        esz = 4 if dtype == F32 else 2
        size = (n * esz + 63) // 64 * 64
        off = self.sp_
        self.sp_ += size
        assert self.sp_ <= ARENA_BYTES, f"SBUF arena overflow {self.sp_} ({name})"
        if dtype == F32:
            base = self.arena_f[0:shape[0], off // 4: off // 4 + n]
        else:
            base = self.arena[0:shape[0], off // 2: off // 2 + n]
        if len(shape) > 2:
            names = [f"d{i}" for i in range(len(shape) - 1)]
            pat = "p (" + " ".join(names) + ") -> p " + " ".join(names)
            base = base.rearrange(pat, **{nm: s for nm, s in zip(names[:-1], shape[1:-1])})
        return base

    def push(self):
        self.stack.append(self.sp_)

    def pop(self):
        self.barrier()
        self.sp_ = self.stack.pop()

    def barrier(self):
        for E in self.ops:
            for e2 in COMPUTE:
                if e2 != E and self.cnt[e2] > 0:
                    self._wait(E, ("c", e2, self.epoch[e2], self.cnt[e2]))
            for q in DMAQ:
                for idx in range(max(0, self.dcnt[q] - DK), self.dcnt[q]):
                    self._wait(E, ("d", q, idx))

    def pbank(self, bf=False):
        i = self.ps_i % 8
        self.ps_i += 1
        return (self.psum_bf[i][:, 0:1024] if bf else self.psum[i][:, 0:512]), f"psum{i}"


class SRot:
    def __init__(self, P, shape, dtype, n, name):
        self.t = [P.sb(shape, dtype, name) for _ in range(n)]
        self.keys = [f"{name}#{i}" for i in range(n)]
        self.i = 0

    def next(self):
        i = self.i % len(self.t)
        self.i += 1
        return self.t[i], self.keys[i]


MAIN_COLS = list(range(0, 3072)) + list(range(3088, 5136)) + list(range(5152, 8224))
SMALL_COLS = list(range(3072, 3088)) + list(range(5136, 5152))
NCST = 8


def _chunked_w(w, ncol_chunk=128):
    K, N = w.shape
    return np.ascontiguousarray(w.reshape(K // 128, 128, N // ncol_chunk, ncol_chunk).transpose(2, 1, 0, 3))


def _pvec(v):
    sh = v.shape
    a = v.reshape(*sh[:-1], sh[-1] // 128, 128)
    return np.ascontiguousarray(np.moveaxis(a, -1, 0))


def host_consts():
    c = np.zeros((128, NCST, 128), np.float32)
    k = np.arange(128)[:, None]
    j = np.arange(128)[None, :]
    c[:, 0, :] = (k == j)
    c[:, 1, :] = 1.0
    c[:, 2, :] = (k <= j)
    c[:, 3, :] = (k >= j)
    c[:, 4, :] = (k < j)
    c[:, 5, :] = (k > j)
    return c


def _bf16(a):
    return np.ascontiguousarray(a.astype(ml_dtypes.bfloat16))


def hyena_tables(L):
    nt = L // 128
    t = np.arange(L, dtype=np.float64)[:, None]
    w = 2.0 * np.pi * (np.arange(L, dtype=np.float64)[None, :] + 0.5) / (2 * L)
    C = np.cos(t * w)
    S = np.sin(t * w)
    def fwd(M):
        return M.reshape(nt, 128, nt, 128).transpose(2, 1, 0, 3)
    fw = np.stack([fwd(C), fwd(S)])
    def inv(M):
        return (M.T / L).reshape(nt, 128, nt, 128).transpose(2, 1, 0, 3)
    iv = np.concatenate([inv(C), inv(S)], axis=2)
    tt_ = np.linspace(0.0, 1.0, L, dtype=np.float32)[:, None]
    ww = (2.0 * math.pi * np.arange(L, dtype=np.float32)[:, None] / L).astype(np.float32)
    ff = np.linspace(1e-4, 15, 16, dtype=np.float32)[None, :]
    z = np.concatenate([tt_, np.cos(ff * ww), -np.sin(ff * ww)], axis=-1).astype(np.float32)
    min_decay = math.log(1e-2) / 1.5
    max_decay = math.log(1e-2) / 0.3
    deltas = np.linspace(min_decay, max_decay, 512, dtype=np.float32)
    win = np.exp(-tt_ * np.abs(deltas)[None, :]).astype(np.float32)
    win1 = win.copy()
    win1[0, :] = 0.0
    wn = np.stack([win, win1]).reshape(2, nt, 128, 512).transpose(0, 2, 1, 3)
    return _bf16(fw), _bf16(iv), np.ascontiguousarray(z.T), np.ascontiguousarray(wn.astype(np.float32))


def prep_shared(inp):
    f = lambda a: np.ascontiguousarray(np.asarray(a, dtype=np.float32))
    sh = {}
    sh["wada"] = np.stack([_chunked_w(f(inp["w_ada"][l])) for l in range(NL)])
    sh["bada"] = _pvec(f(inp["b_ada"]))
    sh["n1w"] = _pvec(f(inp["norm1_w"]))
    sh["n2w"] = _pvec(f(inp["norm2_w"]))
    sh["fnw"] = _pvec(f(inp["final_norm_w"]))
    win = f(inp["w_in"])
    sh["win"] = np.stack([_chunked_w(win[l][:, MAIN_COLS]) for l in range(NL)])
    sh["wsm"] = np.ascontiguousarray(win[:, :, SMALL_COLS].reshape(NL, 8, 128, 32).transpose(0, 2, 1, 3))
    sh["hycw"] = np.ascontiguousarray(_pvec(f(inp["hy_conv_w"])).transpose(0, 1, 3, 2))
    sh["hycb"] = _pvec(f(inp["hy_conv_b"]))
    sh["sscw"] = np.ascontiguousarray(_pvec(f(inp["ssm_conv_w"])).transpose(0, 1, 3, 2))
    sh["sscb"] = _pvec(f(inp["ssm_conv_b"]))
    sh["gdcw"] = np.ascontiguousarray(_pvec(f(inp["gdn_conv_w"])).transpose(0, 1, 3, 2))
    wb = np.stack([f(inp["w_hy_out"]), f(inp["w_ssm_out"]), f(inp["w_gdn_out"])], 1)
    wb = wb.reshape(NL, 3, 4, 128, 8, 128)
    sh["wbr"] = np.ascontiguousarray(wb.transpose(0, 4, 3, 1, 2, 5))
    sh["wout"] = np.stack([_chunked_w(f(inp["w_out"][l])) for l in range(NL)])
    sh["wgu"] = np.stack([_chunked_w(f(inp["w_gate_up"][l])) for l in range(NL)])
    sh["wdn"] = np.stack([_chunked_w(f(inp["w_down"][l])) for l in range(NL)])
    sh["cst"] = host_consts()
    bc = lambda a: np.ascontiguousarray(np.broadcast_to(a[None], (128,) + a.shape))
    sh["ssdtb"] = bc(f(inp["ssm_dt_bias"]).reshape(NL, 16))
    sh["ssAl"] = bc(f(inp["ssm_A_log"]).reshape(NL, 16))
    Dch = np.repeat(f(inp["ssm_D"]), 64, axis=1)
    sh["ssDpc"] = _pvec(Dch)
    sh["ssnw"] = _pvec(f(inp["ssm_norm_w"]))
    for nm, L in (("lat", NLAT), ("ctx", NCTX)):
        fw, iv, zT, wn = hyena_tables(L)
        sh["hfw_" + nm], sh["hiv_" + nm], sh["hz_" + nm], sh["hwin_" + nm] = fw, iv, zT, wn
    sh["hyw1"] = f(inp["hy_w1"])
    sh["hyw2"] = f(inp["hy_w2"])
    sh["hyw3"] = f(inp["hy_w3"])
    sh["hyb"] = np.ascontiguousarray(np.stack([f(inp["hy_b1"]), f(inp["hy_b2"])], -1))
    sh["hyfreq"] = np.ascontiguousarray(f(inp["hy_freq"]).transpose(0, 2, 1))
    sh["hybias"] = bc(f(inp["hy_bias"]))
    sh["gddtb"] = bc(f(inp["gdn_dt_bias"]).reshape(NL, 8))
    sh["gdAl"] = bc(f(inp["gdn_A_log"]).reshape(NL, 8))
    sh["gdnw"] = np.ascontiguousarray(f(inp["gdn_norm_w"]).T)
    return sh


def prep_core(inp, b):
    xin = np.concatenate([np.asarray(inp["ctx"][b], np.float32), np.asarray(inp["x"][b], np.float32)], 0)
    xT0 = np.ascontiguousarray(xin.T.reshape(8, 128, T).transpose(1, 0, 2))
    cT = np.stack([np.asarray(inp["c"][b], np.float32).reshape(8, 128).T,
                   np.asarray(inp["c_ctx"], np.float32).reshape(8, 128).T], -1)
    return {"xT0": xT0, "cT": np.ascontiguousarray(cT)}


def build_program(nlayers=NL, dbg=False, mixers=("hy", "ssm", "gdn")):
    nc = bass.Bass("TRN2", target_bir_lowering=False)

    def IN(name, shape, dt=F32):
        return nc.dram_tensor(name, list(shape), dt, kind="ExternalInput").ap()

    def SCR(name, shape, dt):
        return nc.dram_tensor(name, list(shape), dt, kind="Internal").ap()

    def OUT(name, shape, dt=F32):
        return nc.dram_tensor(name, list(shape), dt, kind="ExternalOutput").ap()

    d_xT0 = IN("xT0", [128, 8, T])
    d_cT = IN("cT", [128, 8, 2])
    d_wada = IN("wada", [NL, 48, 128, 8, 128])
    d_bada = IN("bada", [128, NL, 48])
    d_n1w = IN("n1w", [128, NL, 8])
    d_n2w = IN("n2w", [128, NL, 8])
    d_fnw = IN("fnw", [128, 8])
    d_win = IN("win", [NL, 64, 128, 8, 128])
    d_wsm = IN("wsm", [NL, 128, 8, 32])
    d_hycw = IN("hycw", [128, NL, 12, 3])
    d_hycb = IN("hycb", [128, NL, 12])
    d_sscw = IN("sscw", [128, NL, 8, 3])
    d_sscb = IN("sscb", [128, NL, 8])
    d_gdcw = IN("gdcw", [128, NL, 12, 3])
    d_wbr = IN("wbr", [NL, 8, 128, 3, 4, 128])
    d_wout = IN("wout", [NL, 8, 128, 8, 128])
    d_wgu = IN("wgu", [NL, 44, 128, 8, 128])
    d_wdn = IN("wdn", [NL, 8, 128, 22, 128])
    d_cst = IN("cst", [128, NCST, 128])
    d_ssdtb = IN("ssdtb", [128, NL, 16])
    d_ssAl = IN("ssAl", [128, NL, 16])
    d_ssDpc = IN("ssDpc", [128, NL, 4])
    d_ssnw = IN("ssnw", [128, NL, 4])
    d_gddtb = IN("gddtb", [128, NL, 8])
    HY = {}
    for nm, L in (("lat", NLAT), ("ctx", NCTX)):
        nt_ = L // 128
        HY[nm] = dict(L=L, nt=nt_,
                      fw=IN("hfw_" + nm, [2, nt_, 128, nt_, 128], BF16),
                      iv=IN("hiv_" + nm, [nt_, 128, 2 * nt_, 128], BF16),
                      z=IN("hz_" + nm, [33, L]),
                      win=IN("hwin_" + nm, [2, 128, nt_, 512]),
                      K=SCR("s_K_" + nm, [2, 2, nt_, 128, 512], F32),
                      tile0=(2 if nm == "lat" else 0))
    d_hyw1 = IN("hyw1", [NL, 33, 64])
    d_hyw2 = IN("hyw2", [NL, 64, 64])
    d_hyw3 = IN("hyw3", [NL, 64, 2048])
    d_hyb = IN("hyb", [NL, 64, 2])
    d_hyfreq = IN("hyfreq", [NL, 64, 2])
    d_hybias = IN("hybias", [128, NL, 2, 512])
    d_gdAl = IN("gdAl", [128, NL, 8])
    d_gdnw = IN("gdnw", [128, NL])
    s_go = SCR("s_go", [2, 36, 64, 512], F32)
    d_outT = OUT("outT", [128, 8, NLAT])

    s_hy = SCR("s_hy", [12, 128, T], F32)
    s_sx = SCR("s_sx", [4, 128, T], F32)
    s_sbc = SCR("s_sbc", [4, 128, T], BF16)
    s_sz = SCR("s_sz", [4, 128, T], BF16)
    s_gq = SCR("s_gq", [4, 128, T], BF16)
    s_gk = SCR("s_gk", [4, 128, T], BF16)
    s_gv = SCR("s_gv", [4, 128, T], BF16)
    s_gg = SCR("s_gg", [4, 128, T], BF16)
    s_gate = SCR("s_gate", [24, 128, T], BF16)
    s_small = SCR("s_small", [32, T], F32)
    s_y = SCR("s_y", [12, 128, T], BF16)

    dbg_out = {}

    def DBG(name, shape, dt=F32):
        dbg_out[name] = OUT(name, shape, dt)
        return dbg_out[name]

    P = KProg(nc)
    mm, act, tt, ts, stt, cp, dma = P.mm, P.act, P.tt, P.ts, P.stt, P.cp, P.dma

    cst = P.sb([128, NCST, 128], F32, "cst")
    ident = cst[:, 0, :]
    ones = cst[:, 1, :]
    cstb = P.sb([128, 2, 128], BF16, "cstb")
    identb = cstb[:, 0, :]
    onesb = cstb[:, 1, :]
    xT = P.sb([128, 8, T], F32, "xT")
    modT = P.sb([128, NL, 48, 2], F32, "modT")
    g1 = P.sb([128, NL, 8, 2], F32, "g1")
    g2 = P.sb([128, NL, 8, 2], F32, "g2")
    sT = P.sb([128, 8, 2], F32, "sT")
    bada = P.sb([128, NL, 48], F32, "bada")
    n1w = P.sb([128, NL, 8], F32, "n1w")
    n2w = P.sb([128, NL, 8], F32, "n2w")
    fnw = P.sb([128, 8], F32, "fnw")
    hycw = P.sb([128, NL, 12, 3], F32, "hycw")
    hycb = P.sb([128, NL, 12], F32, "hycb")
    sscw = P.sb([128, NL, 8, 3], F32, "sscw")
    sscb = P.sb([128, NL, 8], F32, "sscb")
    gdcw = P.sb([128, NL, 12, 3], F32, "gdcw")
    ssdtb = P.sb([128, NL, 16], F32, "ssdtb")
    ssA = P.sb([128, NL, 16], F32, "ssA")
    ssDpc = P.sb([128, NL, 4], F32, "ssDpc")
    ssnw = P.sb([128, NL, 4], F32, "ssnw")
    gddtb = P.sb([128, NL, 8], F32, "gddtb")
    gdA = P.sb([128, NL, 8], F32, "gdA")
    gdnw = P.sb([128, NL], F32, "gdnw")
    gmask = P.sb([64, 4, 4, 64], F32, "gmask")
    gI4 = P.sb([64, 4, 64], F32, "gI4")

    dma(cst, d_cst, [], ["cst"])
    cp(cstb, cst[:, 0:2, :], ["cst"], ["cstb"])
    for c in range(8):
        dma(xT[:, c, :], d_xT0[:, c, :], [], [("xT", c)])
    dma(sT, d_cT, [], ["sT"])
    for nm, sbt, dt_ in (("bada", bada, d_bada), ("n1w", n1w, d_n1w), ("n2w", n2w, d_n2w), ("fnw", fnw, d_fnw),
                         ("hycw", hycw, d_hycw), ("hycb", hycb, d_hycb), ("sscw", sscw, d_sscw),
                         ("sscb", sscb, d_sscb), ("gdcw", gdcw, d_gdcw), ("ssdtb", ssdtb, d_ssdtb),
                         ("ssA", ssA, d_ssAl), ("ssDpc", ssDpc, d_ssDpc), ("ssnw", ssnw, d_ssnw),
                         ("gddtb", gddtb, d_gddtb), ("gdA", gdA, d_gdAl), ("gdnw", gdnw, d_gdnw)):
        dma(sbt, dt_, [], [nm])
    act(gdA, gdA, AF.Exp, ["gdA"], ["gdA"])
    ts(gdA, gdA, -1.0, None, ALU.mult, None, ["gdA"], ["gdA"])
    for mi, ci in enumerate((5, 4, 2, 3)):
        for h in range(4):
            cp(gmask[:, mi, h, :], cst[0:64, ci, 0:64], ["cst"], ["gmask"])
    for h in range(4):
        cp(gI4[:, h, :], cst[0:64, 0, 0:64], ["cst"], ["gI4"])
    act(ssA, ssA, AF.Exp, ["ssA"], ["ssA"])
    ts(ssA, ssA, -1.0, None, ALU.mult, None, ["ssA"], ["ssA"])
    act(sT, sT, AF.Silu, ["sT"], ["sT"])

    def ada_slab(l, j, slab_rot, q):
        slab, sk = slab_rot.next()
        dma(slab, d_wada[l, j], [], [sk], q=q)
        ps, pk = P.pbank()
        for kc in range(8):
            mm(ps[:, 0:2], slab[:, kc, :], sT[:, kc, :], kc == 0, kc == 7, [sk, "sT"], [pk])
        ts(modT[:, l, j, :], ps[:, 0:2], bada[:, l, j:j + 1], None, ALU.add, None, [pk, "bada"], [("modT", l)])

    def ada_finish(l):
        for w in range(2):
            ts(g1[:, l, :, w], modT[:, l, 8:16, w], 1.0, None, ALU.add, None, [("modT", l)], [("g1", l)])
            tt(g1[:, l, :, w], g1[:, l, :, w], n1w[:, l, :], ALU.mult, [("g1", l), "n1w"], [("g1", l)])
            ts(g2[:, l, :, w], modT[:, l, 32:40, w], 1.0, None, ALU.add, None, [("modT", l)], [("g2", l)])
            tt(g2[:, l, :, w], g2[:, l, :, w], n2w[:, l, :], ALU.mult, [("g2", l), "n2w"], [("g2", l)])

    P.push()
    slab_rot0 = SRot(P, [128, 8, 128], F32, 3, "adaslab")
    for l_ in range(nlayers):
        for j in range(48):
            ada_slab(l_, j, slab_rot0, "sp" if j % 2 == 0 else "act")
        ada_finish(l_)
    P.pop()
    if dbg:
        dma(DBG("dbg_mod", [128, NL, 48, 2]), modT, [("modT", l_) for l_ in range(NL)], ["dbg_mod"], is_out=True)

    def norm_mod(hT, gsel, shsel, tiles=TT, out_key="hT"):
        sq_rot = SRot(P, [128, 512], BF16, 3, "nsq")
        tmp_rot = SRot(P, [128, 512], F32, 3, "ntmp")
        rstd_rot = SRot(P, [128, 512], F32, 2, "nrstd")
        for (t0, n, w) in tiles:
            ps, pk = P.pbank()
            for c in range(8):
                sq, sqk = sq_rot.next()
                act(sq[:, :n], xT[:, c, t0:t0 + n], AF.Square, [("xT", c)], [sqk])
                mm(ps[:, :n], onesb, sq[:, :n], c == 0, c == 7, ["cstb", sqk], [pk])
            rstd, rk = rstd_rot.next()
            ts(rstd[:, :n], ps[:, :n], 1.0 / D, EPS, ALU.mult, ALU.add, [pk], [rk])
            act(rstd[:, :n], rstd[:, :n], AF.Sqrt, [rk], [rk])
            P.emit("dve", lambda e, o=rstd[:, :n]: e.reciprocal(o, o), [rk], [rk])
            for c in range(8):
                tmp, tk = tmp_rot.next()
                stt(tmp[:, :n], xT[:, c, t0:t0 + n], gsel(c, w), rstd[:, :n], ALU.mult, ALU.mult,
                    [("xT", c), rk] + [(nm_, l_) for nm_ in ("g1", "g2") for l_ in range(NL)] + ["fnw"], [tk])
                if shsel is not None:
                    act(hT[:, c, t0:t0 + n], tmp[:, :n], AF.Identity, [tk] + [("modT", l_) for l_ in range(NL)], [(out_key, c)], bias=shsel(c, w))
                else:
                    act(hT[:, c, t0:t0 + n], tmp[:, :n], AF.Copy, [tk], [(out_key, c)])

    def conv3(dst, src, w3, rows_lat=32):
        ops_ = []
        segs = [(src[:, 0:NCTX].rearrange("p (r t) -> p r t", r=1), dst[:, 0:NCTX].rearrange("p (r t) -> p r t", r=1), NCTX),
                (src[:, NCTX:T].rearrange("p (r t) -> p r t", r=rows_lat), dst[:, NCTX:T].rearrange("p (r t) -> p r t", r=rows_lat), 64)]
        for s3, d3, rl in segs:
            ops_.append(("ts", d3, s3, w3[:, 1:2]))
            ops_.append(("stt", d3[:, :, 1:rl], s3[:, :, 0:rl - 1], w3[:, 0:1], d3[:, :, 1:rl]))
            ops_.append(("stt", d3[:, :, 0:rl - 1], s3[:, :, 1:rl], w3[:, 2:3], d3[:, :, 0:rl - 1]))
        return ops_

    def run_conv(dst, dk_, src, sk_, w3, wkey, eng="dve"):
        for op in conv3(dst, src, w3):
            if op[0] == "ts":
                ts(op[1], op[2], op[3], None, ALU.mult, None, [sk_, wkey], [dk_], eng=eng)
            else:
                stt(op[1], op[2], op[3], op[4], ALU.mult, ALU.add, [sk_, wkey, dk_], [dk_], eng=eng)

    def layer(l):
        P.push()
        hT = P.sb([128, 8, T], BF16, "hT")
        P.push()
        norm_mod(hT, lambda c, w: g1[:, l, c, w:w + 1], lambda c, w: modT[:, l, c, w:w + 1])
        P.pop()
        hkeys = [("hT", c) for c in range(8)]
        P.push()
        wslab_rot = SRot(P, [128, 8, 128], BF16, 3, "wslab")
        pc_rot = SRot(P, [128, T], F32, 2, "pc")
        cv_rot = SRot(P, [128, T], F32, 2, "cv")
        ob_rot = SRot(P, [128, T], BF16, 2, "ob")
        l2_rot = SRot(P, [128, 512], F32, 2, "l2t")
        ada_rot = SRot(P, [128, 8, 128], F32, 3, "adaslab2")
        ada_next = [0]

        def ada_step():
            if ADA_INTERLEAVE and l + 1 < nlayers and ada_next[0] < 48:
                ada_slab(l + 1, ada_next[0], ada_rot, "sp")
                ada_next[0] += 1
                if ada_next[0] == 48:
                    ada_finish(l + 1)

        def proj(j, consumer, ncols=128, small=False):
            slab, sk = wslab_rot.next()
            if small:
                dma(slab[:, :, 0:32], d_wsm[l], [], [sk], q="pool")
            else:
                dma(slab, d_win[l, j], [], [sk], q="pool")
            for (t0, n, w) in TT:
                ps, pk = P.pbank()
                for kc in range(8):
                    mm(ps[0:ncols, :n], slab[:, kc, 0:ncols], hT[:, kc, t0:t0 + n], kc == 0, kc == 7, [sk, hkeys[kc]], [pk])
                consumer(ps, pk, t0, n)
            ada_step()

        def conv_chunk(j, w3, wkey, bias, bkey, func, dst_dram, dst_f32, post=None):
            pc, pck = pc_rot.next()
            proj(j, lambda ps, pk, t0, n: act(pc[:, t0:t0 + n], ps[:, :n], AF.Copy, [pk], [pck]))
            cv, cvk = cv_rot.next()
            run_conv(cv, cvk, pc, pck, w3, wkey)
            if post is not None:
                post(cv, cvk, dst_dram)
                return
            if dst_f32:
                kw = {"bias": bias} if bias is not None else {}
                act(cv, cv, func, [cvk, bkey], [cvk], **kw)
                dma(dst_dram, cv, [cvk], [("scr", str(dst_dram))])
            else:
                ob, obk = ob_rot.next()
                kw = {"bias": bias} if bias is not None else {}
                act(ob, cv, func, [cvk, bkey], [obk], **kw)
                dma(dst_dram, ob, [obk], [("scr", str(dst_dram))])

        def plain_chunk(j, func, dst_dram):
            ob, obk = ob_rot.next()
            proj(j, lambda ps, pk, t0, n: act(ob[:, t0:t0 + n], ps[:, :n], func, [pk], [obk]))
            dma(dst_dram, ob, [obk], [("scr", str(dst_dram))])

        def l2norm_post(scale):
            def post(cv, cvk, dst_dram):
                act(cv, cv, AF.Silu, [cvk], [cvk])
                ob, obk = ob_rot.next()
                for (t0, n, w) in TT:
                    sq, sqk = l2_rot.next()
                    act(sq[:, :n], cv[:, t0:t0 + n], AF.Square, [cvk], [sqk])
                    ps, pk = P.pbank()
                    mm(ps[:, :n], ones, sq[:, :n], True, True, ["cst", sqk], [pk])
                    ts(sq[:, :n], ps[:, :n], EPS, None, ALU.add, None, [pk], [sqk])
                    act(sq[:, :n], sq[:, :n], AF.Sqrt, [sqk], [sqk])
                    P.emit("dve", lambda e, o=sq[:, :n]: e.reciprocal(o, o), [sqk], [sqk])
                    stt(ob[:, t0:t0 + n], cv[:, t0:t0 + n], scale, sq[:, :n], ALU.mult, ALU.mult, [cvk, sqk], [obk])
                dma(dst_dram, ob, [obk], [("scr", str(dst_dram))])
            return post

        for j in range(12):
            conv_chunk(j, hycw[:, l, j, :], "hycw", hycb[:, l, j:j + 1], "hycb", AF.Identity, s_hy[j], True)
        for j in range(4):
            plain_chunk(12 + j, AF.Silu, s_sz[j])
        for j in range(8):
            if j < 4:
                conv_chunk(16 + j, sscw[:, l, j, :], "sscw", sscb[:, l, j:j + 1], "sscb", AF.Silu, s_sx[j], True)
            else:
                conv_chunk(16 + j, sscw[:, l, j, :], "sscw", sscb[:, l, j:j + 1], "sscb", AF.Silu, s_sbc[j - 4], False)
        for j in range(12):
            if j < 4:
                conv_chunk(24 + j, gdcw[:, l, j, :], "gdcw", None, "gdcw", None, s_gq[j], False, post=l2norm_post(128.0 ** -0.5))
            elif j < 8:
                conv_chunk(24 + j, gdcw[:, l, j, :], "gdcw", None, "gdcw", None, s_gk[j - 4], False, post=l2norm_post(1.0))
            else:
                conv_chunk(24 + j, gdcw[:, l, j, :], "gdcw", None, "gdcw", AF.Silu, s_gv[j - 8], False)
        for j in range(4):
            plain_chunk(36 + j, AF.Silu, s_gg[j])
        for j in range(24):
            plain_chunk(40 + j, AF.Sigmoid, s_gate[j])
        sm, smk = pc_rot.next()
        proj(0, lambda ps, pk, t0, n: act(sm[0:32, t0:t0 + n], ps[0:32, :n], AF.Copy, [pk], [smk]), ncols=32, small=True)
        dma(s_small, sm[0:32, :], [smk], [("scr", str(s_small))])
        while ADA_INTERLEAVE and l + 1 < nlayers and ada_next[0] < 48:
            ada_step()
        P.pop()
        P.pop()

        if dbg and l == 0:
            P.push()
            for nm, src, nchunk, dt_ in (("dbg_hy", s_hy, 12, F32), ("dbg_sx", s_sx, 4, F32), ("dbg_sbc", s_sbc, 4, BF16),
                                         ("dbg_sz", s_sz, 4, BF16), ("dbg_gq", s_gq, 4, BF16), ("dbg_gk", s_gk, 4, BF16),
                                         ("dbg_gv", s_gv, 4, BF16), ("dbg_gg", s_gg, 4, BF16), ("dbg_gate", s_gate, 24, BF16)):
                o = DBG(nm, [nchunk, 128, T], dt_)
                for j in range(nchunk):
                    dma(o[j], src[j], [("scr", str(src[j]))], [nm + str(j)], is_out=True)
            o = DBG("dbg_small", [32, T], F32)
            dma(o, s_small, [("scr", str(s_small))], ["dbg_small"], is_out=True)
            P.pop()

        if "hy" in mixers:
            hyena(l)
        else:
            for j in range(4):
                dma(s_y[j], s_gk[j], [("scr", str(s_gk[j]))], [("scr", str(s_y[j]))])
        if "ssm" in mixers:
            ssd(l)
        else:
            for j in range(4):
                dma(s_y[4 + j], s_sbc[j], [("scr", str(s_sbc[j]))], [("scr", str(s_y[4 + j]))])
        if "gdn" in mixers:
            gdn(l)
        else:
            for j in range(4):
                dma(s_y[8 + j], s_gv[j], [("scr", str(s_gv[j]))], [("scr", str(s_y[8 + j]))])
        if dbg and l == 0:
            P.push()
            o = DBG("dbg_y", [12, 128, T], BF16)
            for j in range(12):
                dma(o[j], s_y[j], [("scr", str(s_y[j]))], ["dbg_y" + str(j)], is_out=True)
            P.pop()

        TTm = TT[1:] if l == nlayers - 1 else TT
        halves = (TTm[0:len(TTm) - 2], TTm[len(TTm) - 2:])
        P.push()
        yT = P.sb([128, 12, T], BF16, "yT")
        mT = P.sb([128, 8, T], BF16, "mT")
        for j in range(12):
            dma(yT[:, j, :], s_y[j], [("scr", str(s_y[j]))], [("yT", j)])
        P.push()
        wb_rot = SRot(P, [128, 3, 4, 128], BF16, 2, "wbr")
        gt_rot = SRot(P, [128, 3, 512], BF16, 3, "gt")
        acc_rot = SRot(P, [128, 512], F32, 3, "macc")
        for dch in range(8):
            wb, wbk = wb_rot.next()
            dma(wb, d_wbr[l, dch], [], [wbk], q="pool")
            for (t0, n, w) in TTm:
                gt_, gtk = gt_rot.next()
                for i in range(3):
                    dma(gt_[:, i, 0:n], s_gate[8 * i + dch][:, t0:t0 + n], [("scr", str(s_gate[8 * i + dch]))], [(gtk, i)])
                acc, ak = acc_rot.next()
                for i in range(3):
                    ps, pk = P.pbank()
                    for kc in range(4):
                        mm(ps[:, :n], wb[:, i, kc, :], yT[:, 4 * i + kc, t0:t0 + n], kc == 0, kc == 3, [wbk, ("yT", 4 * i + kc)], [pk])
                    if i == 0:
                        tt(acc[:, :n], ps[:, :n], gt_[:, i, 0:n], ALU.mult, [pk, (gtk, i)], [ak])
                    else:
                        tmp, tk = acc_rot.next()
                        tt(tmp[:, :n], ps[:, :n], gt_[:, i, 0:n], ALU.mult, [pk, (gtk, i)], [tk])
                        if i == 1:
                            tt(acc[:, :n], acc[:, :n], tmp[:, :n], ALU.add, [ak, tk], [ak], eng="dve")
                        else:
                            tt(mT[:, dch, t0:t0 + n], acc[:, :n], tmp[:, :n], ALU.add, [ak, tk], [("mT", dch)], eng="dve")
        P.pop()
        wo_rot = SRot(P, [128, 8, 128], BF16, 2, "wo")
        for och in range(8):
            wo, wok = wo_rot.next()
            dma(wo, d_wout[l, och], [], [wok], q="pool")
            for (t0, n, w) in TTm:
                ps, pk = P.pbank()
                for kc in range(8):
                    mm(ps[:, :n], wo[:, kc, :], mT[:, kc, t0:t0 + n], kc == 0, kc == 7, [wok, ("mT", kc)], [pk])
                stt(xT[:, och, t0:t0 + n], ps[:, :n], modT[:, l, 16 + och, w:w + 1], xT[:, och, t0:t0 + n], ALU.mult, ALU.add,
                    [pk, ("modT", l), ("xT", och)], [("xT", och)])
        P.pop()

        P.push()
        hT2 = P.sb([128, 8, T], BF16, "hT2")
        P.push()
        norm_mod(hT2, lambda c, w: g2[:, l, c, w:w + 1], lambda c, w: modT[:, l, 24 + c, w:w + 1], tiles=TTm, out_key="hT2")
        P.pop()
        h2keys = [("hT2", c) for c in range(8)]
        for half in halves:
            P.push()
            h0 = half[0][0]
            hn = sum(tl[1] for tl in half)
            aT = P.sb([128, 22, hn], BF16, "aT")
            P.push()
            wg_rot = SRot(P, [128, 8, 128], BF16, 4, "wgu")
            sg_rot = SRot(P, [128, 512], F32, 3, "sg")
            for j in range(22):
                wg, wgk = wg_rot.next()
                dma(wg, d_wgu[l, j], [], [wgk], q="pool")
                wu, wuk = wg_rot.next()
                dma(wu, d_wgu[l, 22 + j], [], [wuk], q="pool")
                for (t0, n, w) in half:
                    psg, pgk = P.pbank()
                    for kc in range(8):
                        mm(psg[:, :n], wg[:, kc, :], hT2[:, kc, t0:t0 + n], kc == 0, kc == 7, [wgk, h2keys[kc]], [pgk])
                    psu, puk = P.pbank()
                    for kc in range(8):
                        mm(psu[:, :n], wu[:, kc, :], hT2[:, kc, t0:t0 + n], kc == 0, kc == 7, [wuk, h2keys[kc]], [puk])
                    sg, sgk = sg_rot.next()
                    act(sg[:, :n], psg[:, :n], AF.Silu, [pgk], [sgk])
                    tt(aT[:, j, t0 - h0:t0 - h0 + n], sg[:, :n], psu[:, :n], ALU.mult, [sgk, puk], [("aT", j)])
            P.pop()
            wd_rot = SRot(P, [128, 22, 128], BF16, 2, "wdn")
            for och in range(8):
                wd, wdk = wd_rot.next()
                dma(wd, d_wdn[l, och], [], [wdk], q="pool")
                for (t0, n, w) in half:
                    ps, pk = P.pbank()
                    for kc in range(22):
                        mm(ps[:, :n], wd[:, kc, :], aT[:, kc, t0 - h0:t0 - h0 + n], kc == 0, kc == 21, [wdk, ("aT", kc)], [pk])
                    stt(xT[:, och, t0:t0 + n], ps[:, :n], modT[:, l, 40 + och, w:w + 1], xT[:, och, t0:t0 + n], ALU.mult, ALU.add,
                        [pk, ("modT", l), ("xT", och)], [("xT", och)])
            P.pop()
        P.pop()

    def hyena(l):
        raise NotImplementedError

    def ssd(l):
        raise NotImplementedError

    def gdn(l):
        raise NotImplementedError

    MIXER_IMPL = {}

    def hyena_impl(l):
        MAGIC = 12582912.0
        TWO_PI = 2.0 * math.pi
        P.push()
        w1s = P.sb([33, 64], F32, "hw1")
        w2s = P.sb([64, 64], F32, "hw2")
        w3s = P.sb([64, 2048], F32, "hw3")
        hb = P.sb([64, 2], F32, "hb")
        hfr = P.sb([64, 2], F32, "hfr")
        hfb = P.sb([64, 2], F32, "hfb")
        dma(w1s, d_hyw1[l], [], ["hw1"])
        dma(w2s, d_hyw2[l], [], ["hw2"])
        dma(w3s, d_hyw3[l], [], ["hw3"])
        dma(hb, d_hyb[l], [], ["hb"])
        dma(hfr, d_hyfreq[l], [], ["hfr"])
        tt(hfb, hfr, hb, ALU.mult, ["hfr", "hb"], ["hfb"])
        t_rot = SRot(P, [128, 512], F32, 4, "hft")

        def sin_mod(dst, dkey, src, skeys, k, n):
            t1, t1k = t_rot.next()
            t2, t2k = t_rot.next()
            ts(t1[0:64, :n], src, hfr[:, k:k + 1], hfb[:, k:k + 1], ALU.mult, ALU.add, skeys + ["hfr", "hfb"], [t1k])
            ts(t2[0:64, :n], t1[0:64, :n], 1.0 / TWO_PI, MAGIC, ALU.mult, ALU.add, [t1k], [t2k])
            ts(t2[0:64, :n], t2[0:64, :n], MAGIC, -TWO_PI, ALU.subtract, ALU.mult, [t2k], [t2k])
            tt(t1[0:64, :n], t1[0:64, :n], t2[0:64, :n], ALU.add, [t1k, t2k], [t1k])
            ts(t1[0:64, :n], t1[0:64, :n], math.pi, -math.pi, ALU.min, ALU.max, [t1k], [t1k])
            act(dst, t1[0:64, :n], AF.Sin, [t1k], [dkey])

        hy_names = ("lat",) if l == nlayers - 1 else ("ctx", "lat")
        for nm in hy_names:
            H = HY[nm]
            L, nt = H["L"], H["nt"]
            P.push()
            zT = P.sb([33, L], F32, "hzT")
            dma(zT, H["z"], [], ["hzT"])
            h1T = P.sb([64, L], F32, "h1T")
            h2T = P.sb([64, L], F32, "h2T")
            nn = min(512, L)
            for t0 in range(0, L, nn):
                ps, pk = P.pbank()
                mm(ps[0:64, :nn], w1s, zT[:, t0:t0 + nn], True, True, ["hw1", "hzT"], [pk])
                sin_mod(h1T[:, t0:t0 + nn], "h1T", ps[0:64, :nn], [pk], 0, nn)
            for t0 in range(0, L, nn):
                ps, pk = P.pbank()
                mm(ps[0:64, :nn], w2s, h1T[:, t0:t0 + nn], True, True, ["hw2", "h1T"], [pk])
                sin_mod(h2T[:, t0:t0 + nn], "h2T", ps[0:64, :nn], [pk], 1, nn)
            hs = P.sb([128, nt, 512], BF16, "hs")
            hd = P.sb([128, nt, 512], BF16, "hd")
            win_rot = SRot(P, [128, 2, 512], F32, 2, "hwin")
            slab_rot = SRot(P, [128, nt, 128], BF16, 4, "hfslab")
            ko_rot = SRot(P, [128, 512], F32, 3, "hko")
            for o in range(2):
                for tt_i in range(nt):
                    wn, wnk = win_rot.next()
                    for dd in range(2):
                        dma(wn[:, dd, :], H["win"][dd, :, tt_i, :], [], [(wnk, dd)])
                    k0, k0k = t_rot.next()
                    k1, k1k = t_rot.next()
                    for dd, (kk, kkk) in enumerate(((k0, k0k), (k1, k1k))):
                        ps, pk = P.pbank()
                        col0 = (o * 2 + dd) * 512
                        mm(ps, h2T[:, tt_i * 128:(tt_i + 1) * 128], w3s[:, col0:col0 + 512], True, True, ["h2T", "hw3"], [pk])
                        tt(kk, ps, wn[:, dd, :], ALU.mult, [pk, (wnk, dd)], [kkk])
                    tt(hs[:, tt_i, :], k0, k1, ALU.add, [k0k, k1k], [("hs", tt_i)], eng="dve")
                    tt(hd[:, tt_i, :], k1, k0, ALU.subtract, [k0k, k1k], [("hd", tt_i)], eng="dve")
                for fc in range(nt):
                    for ri, src, sname in ((0, hs, "hs"), (1, hd, "hd")):
                        slab, slk = slab_rot.next()
                        dma(slab, H["fw"][ri, fc], [], [slk])
                        ps, pk = P.pbank()
                        for tt_i in range(nt):
                            mm(ps, slab[:, tt_i, :], src[:, tt_i, :], tt_i == 0, tt_i == nt - 1, [slk, (sname, tt_i)], [pk])
                        ko, kok = ko_rot.next()
                        cp(ko, ps, [pk], [kok], eng="act")
                        dma(H["K"][o, ri, fc], ko, [kok], [("hK", nm, o, ri, fc)])
            P.pop()
        P.pop()

        P.push()
        vz = P.sb([128, 18, 512], BF16, "vz")
        x12 = P.sb([128, 18, 512], BF16, "x12")
        hbias = P.sb([128, 2, 512], F32, "hbias")
        dma(hbias, d_hybias[:, l, :, :], [], ["hbias"])

        def to_token_major(dst, dname, j0):
            P.push()
            chs = [P.sb([128, T], F32, "hch") for j in range(4)]
            pieces = [(0, 384), (384, 1024), (1024, 1664), (1664, T)]
            for (a_, b_) in pieces:
                for j in range(4):
                    dma(chs[j][:, a_:b_], s_hy[j0 + j][:, a_:b_], [("scr", str(s_hy[j0 + j]))], [("hch", j, a_)])
            piece_of = lambda i: [a_ for (a_, b_) in pieces if a_ <= i * 128 < b_][0]
            for i in range(18):
                ps, pk = P.pbank()
                for j in range(4):
                    P.tr(ps[:, j * 128:(j + 1) * 128], chs[j][:, i * 128:(i + 1) * 128], ident, [("hch", j, piece_of(i)), "cst"], [pk])
                cp(dst[:, i, :], ps, [pk], [(dname, i)], eng=("act" if i % 2 else "dve"))
            P.pop()

        def long_conv(o, last):
            for nm in hy_names:
                H = HY[nm]
                nt, tile0 = H["nt"], H["tile0"]
                P.push()
                Y = P.sb([128, 2 * nt, 512], BF16, "hY")
                P.push()
                slab_rot = SRot(P, [128, nt, 128], BF16, 4, "hcslab")
                kk_rot = SRot(P, [128, 2, 512], F32, 2, "hkk")
                tm_rot = SRot(P, [128, 512], F32, 4, "htm")
                for fc in range(nt):
                    pss = []
                    for ri in range(2):
                        slab, slk = slab_rot.next()
                        dma(slab, H["fw"][ri, fc], [], [slk])
                        ps, pk = P.pbank()
                        for tt_i in range(nt):
                            mm(ps, slab[:, tt_i, :], vz[:, tile0 + tt_i, :], tt_i == 0, tt_i == nt - 1, [slk, ("vz", tile0 + tt_i)], [pk])
                        pss.append((ps, pk))
                    kk, kkk = kk_rot.next()
                    for ri in range(2):
                        dma(kk[:, ri, :], H["K"][o, ri, fc], [("hK", nm, o, ri, fc)], [kkk])
                    (pre, prk), (pim, pik) = pss
                    t1, t1k = tm_rot.next()
                    t2, t2k = tm_rot.next()
                    tt(t1, pre, kk[:, 0, :], ALU.mult, [prk, kkk], [t1k])
                    tt(t2, pim, kk[:, 1, :], ALU.mult, [pik, kkk], [t2k])
                    tt(Y[:, fc, :], t1, t2, ALU.add, [t1k, t2k], [("hY", fc)], eng="dve")
                    t3, t3k = tm_rot.next()
                    t4, t4k = tm_rot.next()
                    tt(t3, pim, kk[:, 0, :], ALU.mult, [pik, kkk], [t3k])
                    tt(t4, pre, kk[:, 1, :], ALU.mult, [prk, kkk], [t4k])
                    tt(Y[:, nt + fc, :], t3, t4, ALU.subtract, [t3k, t4k], [("hY", nt + fc)], eng="dve")
                P.pop()
                P.push()
                iv_rot = SRot(P, [128, 2 * nt, 128], BF16, 2, "hiv")
                ta_rot = SRot(P, [128, 512], F32, 3, "hta")
                yh_rot = SRot(P, [128, 512], BF16, 2, "hyh")
                yo_rot = SRot(P, [128, 4, 128], BF16, 2, "hyo")
                for ti in range(nt):
                    iv, ivk = iv_rot.next()
                    dma(iv, H["iv"][ti], [], [ivk])
                    ps, pk = P.pbank()
                    for k in range(2 * nt):
                        mm(ps, iv[:, k, :], Y[:, k, :], k == 0, k == 2 * nt - 1, [ivk, ("hY", k)], [pk])
                    i = tile0 + ti
                    ta, tak = ta_rot.next()
                    tt(ta, vz[:, i, :], hbias[:, o, :], ALU.mult, [("vz", i), "hbias"], [tak], eng="dve")
                    tt(ta, ta, ps, ALU.add, [tak, pk], [tak])
                    if not last:
                        tt(vz[:, i, :], ta, x12[:, i, :], ALU.mult, [tak, ("x12", i)], [("vz", i)])
                    else:
                        yh, yhk = yh_rot.next()
                        tt(yh, ta, x12[:, i, :], ALU.mult, [tak, ("x12", i)], [yhk])
                        psb, pbk = P.pbank(bf=True)
                        for j in range(4):
                            P.tr(psb[:, j * 128:(j + 1) * 128], yh[:, j * 128:(j + 1) * 128], identb, [yhk, "cstb"], [pbk])
                        yo, yok = yo_rot.next()
                        cp(yo.rearrange("p a b -> p (a b)"), psb[:, 0:512], [pbk], [yok], eng="act")
                        for j in range(4):
                            dma(s_y[j][:, i * 128:(i + 1) * 128], yo[:, j, :], [yok], [("scr_y", j, i)])
                P.pop()
                P.pop()

        to_token_major(vz, "vz", 0)
        to_token_major(x12, "x12", 4)
        long_conv(0, False)
        to_token_major(x12, "x12", 8)
        long_conv(1, True)
        P.pop()

    MIXER_IMPL["hy"] = hyena_impl

    def gdn_impl(l):
        PS = P.psum
        PSB = P.psum_bf
        PK = [f"psum{i}" for i in range(8)]
        NCH = 36
        P.push()
        qT = P.sb([128, 4, T], BF16, "gqT")
        kT = P.sb([128, 4, T], BF16, "gkT")
        vT = P.sb([128, 4, T], BF16, "gvT")
        for h in range(4):
            dma(qT[:, h, :], s_gq[h], [("scr", str(s_gq[h]))], ["gqT"])
            dma(kT[:, h, :], s_gk[h], [("scr", str(s_gk[h]))], ["gkT"])
            dma(vT[:, h, :], s_gv[h], [("scr", str(s_gv[h]))], ["gvT"])
        gt = P.sb([64, NCH, 8], F32, "g_tok")
        beta = P.sb([64, NCH, 8], F32, "g_beta")
        gc = P.sb([64, NCH, 8], F32, "g_gc")
        ngc = P.sb([64, NCH, 8], F32, "g_ngc")
        egc = P.sb([64, NCH, 8], F32, "g_egc")
        bexp = P.sb([64, NCH, 8], F32, "g_bexp")
        kdw = P.sb([64, NCH, 8], F32, "g_kdw")
        gtot = P.sb([128, NCH, 8], F32, "g_gtot")
        etot = P.sb([128, NCH, 8], F32, "g_etot")
        fl = lambda t_: t_.rearrange("p a b -> p (a b)")
        P.push()
        abr = P.sb([16, T], F32, "abr")
        dma(abr, s_small[16:32, :], [("scr", str(s_small))], ["abr"])
        for c in range(NCH):
            P.tr(PS[7][0:64, 0:16], abr[0:16, c * 64:(c + 1) * 64], ident[0:16, 0:16], ["abr", "cst"], [PK[7]])
            tt(gt[:, c, :], PS[7][0:64, 0:8], gddtb[0:64, l, :], ALU.add, [PK[7], "gddtb"], ["g_tok"])
            cp(beta[:, c, :], PS[7][0:64, 8:16], [PK[7]], ["g_beta"])
        P.pop()
        act(fl(gt), fl(gt), AF.Exp, ["g_tok"], ["g_tok"])
        act(fl(gt), fl(gt), AF.Ln, ["g_tok"], ["g_tok"], bias=1.0)
        act(fl(beta), fl(beta), AF.Sigmoid, ["g_beta"], ["g_beta"])
        for c in range(NCH):
            tt(gt[:, c, :], gt[:, c, :], gdA[0:64, l, :], ALU.mult, ["g_tok", "gdA"], ["g_tok"])
        for c in range(NCH):
            mm(PS[7][0:64, 0:4], cst[0:64, 2, 0:64], gt[:, c, 0:4], True, True, ["cst", "g_tok"], [PK[7]])
            mm(PS[7][0:64, 4:8], cst[0:64, 3, 0:64], gt[:, c, 4:8], True, True, ["cst", "g_tok"], [PK[7]])
            mm(PS[7][:, 8:16], cst[0:64, 1, :], gt[:, c, :], True, True, ["cst", "g_tok"], [PK[7]])
            cp(gc[:, c, :], PS[7][0:64, 0:8], [PK[7]], ["g_gc"])
            cp(gtot[:, c, :], PS[7][:, 8:16], [PK[7]], ["g_gtot"])
        ts(fl(ngc), fl(gc), -1.0, None, ALU.mult, None, ["g_gc"], ["g_ngc"])
        act(fl(egc), fl(gc), AF.Exp, ["g_gc"], ["g_egc"])
        tt(fl(bexp), fl(egc), fl(beta), ALU.mult, ["g_egc", "g_beta"], ["g_bexp"])
        tt(fl(kdw), fl(gtot[0:64]), fl(gc), ALU.subtract, ["g_gtot", "g_gc"], ["g_kdw"])
        act(fl(kdw), fl(kdw), AF.Exp, ["g_kdw"], ["g_kdw"])
        act(fl(etot), fl(gtot), AF.Exp, ["g_gtot"], ["g_etot"])

        S = P.sb([128, 2, 4, 128], F32, "gS")
        Sb = P.sb([128, 2, 4, 128], BF16, "gSb")
        for d_ in range(2):
            P.memset(S[:, d_], 0.0, [("gS", d_)])
            P.memset(Sb[:, d_], 0.0, [("gSb", d_)])

        def bcn(ap8, d, n, np_=64):
            return bass.AP(ap8.tensor, ap8.offset + d * 4, [list(ap8.ap[0]), [1, 4], [0, n]])

        f4 = lambda t_: t_.rearrange("p a b -> p (a b)")
        v4 = lambda ap_: ap_.rearrange("p (h k) -> p h k", h=4)
        NSLOT = 2
        prep_n = [0, 0]
        rec_n = [0, 0]
        pools = []
        for d_ in range(2):
            nm = f"g{d_}"
            pools.append(dict(
                kbe=SRot(P, [64, 4, 128], BF16, 1, nm + "kbe"), vb=SRot(P, [64, 4, 128], BF16, 1, nm + "vb"),
                kd=SRot(P, [64, 4, 128], BF16, NSLOT, nm + "kd"), vn=SRot(P, [64, 4, 128], BF16, 1, nm + "vn"),
                sc=SRot(P, [64, 4, 64], F32, 2, nm + "sc"), qkm=SRot(P, [64, 4, 64], BF16, NSLOT, nm + "qkm"),
                X=SRot(P, [64, 4, 64], F32, 2, nm + "X"), Y=SRot(P, [64, 4, 64], F32, 2, nm + "Y"),
                Qf=SRot(P, [64, 4, 64], F32, 2, nm + "Qf"), Qb=SRot(P, [64, 4, 64], BF16, 1, nm + "Qb"),
                wT=SRot(P, [128, 4, 64], BF16, NSLOT, nm + "wT"), u=SRot(P, [64, 4, 128], F32, NSLOT, nm + "u"),
                tA=SRot(P, [64, 4, 128], F32, 1, nm + "tA")))
        handoff = [dict(), dict()]

        def prep(d, order):
            pl = pools[d]
            B0, B1 = 4 * d, 4 * d + 1
            tri = cst[0:64, 2 if d == 0 else 3, 0:64]
            mA = gmask[:, d, :, :]
            mQ = gmask[:, 2 + d, :, :]
            for n_, c in enumerate(order):
                while rec_n[d] < n_ - NSLOT + 1:
                    yield
                csl = slice(c * 64, (c + 1) * 64)
                for h in range(4):
                    P.tr(PSB[B0][0:64, h * 128:(h + 1) * 128], kT[:, h, csl], identb, ["gkT", "cstb"], [PK[B0]])
                for h in range(4):
                    mm(PS[B1][0:64, h * 64:(h + 1) * 64], kT[:, h, csl], kT[:, h, csl], True, True, ["gkT"], [PK[B1]])
                    mm(PS[B1][0:64, 256 + h * 64:256 + (h + 1) * 64], kT[:, h, csl], qT[:, h, csl], True, True, ["gkT", "gqT"], [PK[B1]])
                kbe, kbek = pl["kbe"].next()
                kd, kdk = pl["kd"].next()
                tt(kbe, v4(PSB[B0][0:64, 0:512]), bcn(bexp[:, c, :], d, 128), ALU.mult, [PK[B0], "g_bexp"], [kbek])
                tt(kd, v4(PSB[B0][0:64, 0:512]), bcn(kdw[:, c, :], d, 128), ALU.mult, [PK[B0], "g_kdw"], [kdk])
                yield
                for h in range(4):
                    P.tr(PSB[B0][0:64, h * 128:(h + 1) * 128], vT[:, h, csl], identb, ["gvT", "cstb"], [PK[B0]])
                vb, vbk = pl["vb"].next()
                tt(vb, v4(PSB[B0][0:64, 0:512]), bcn(beta[:, c, :], d, 128), ALU.mult, [PK[B0], "g_beta"], [vbk])
                gU, gUk = pl["sc"].next()
                for h in range(4):
                    act(gU[:, h, :], tri, AF.Copy, ["cst", "g_tok"], [gUk], scale=gt[:, c, d * 4 + h:d * 4 + h + 1])
                yield
                mm(PS[B0][0:64, 0:256], cst[0:64, 1, 0:64], f4(gU), True, True, ["cst", gUk], [PK[B0]])
                R3 = PS[B0][0:64, 0:256].rearrange("p (h k) -> p h k", h=4)
                e, ek = pl["sc"].next()
                tt(e, R3, bcn(ngc[:, c, :], d, 64), ALU.add, [PK[B0], "g_ngc"], [ek])
                ts(f4(e), f4(e), 0.0, None, ALU.min, None, [ek], [ek])
                yield
                act(f4(e), f4(e), AF.Exp, [ek], [ek])
                tt(e, e, mQ, ALU.mult, [ek, "gmask"], [ek], eng="dve")
                yield
                qkm, qkmk = pl["qkm"].next()
                tt(f4(qkm), PS[B1][0:64, 256:512], f4(e), ALU.mult, [PK[B1], ek], [qkmk])
                e2, e2k = pl["sc"].next()
                tt(e2, R3, bcn(ngc[:, c, :], d, 64), ALU.add, [PK[B0], "g_ngc"], [e2k])
                ts(f4(e2), f4(e2), 0.0, None, ALU.max, None, [e2k], [e2k])
                yield
                act(f4(e2), f4(e2), AF.Exp, [e2k], [e2k], scale=-1.0)
                tt(e2, e2, mA, ALU.mult, [e2k, "gmask"], [e2k], eng="dve")
                tt(e2, e2, bcn(beta[:, c, :], d, 64), ALU.mult, [e2k, "g_beta"], [e2k], eng="dve")
                yield
                X, Xk = pl["X"].next()
                tt(f4(X), PS[B1][0:64, 0:256], f4(e2), ALU.mult, [PK[B1], e2k], [Xk])
                yield
                for h in range(4):
                    P.tr(PS[B0][0:64, 256 + h * 64:256 + (h + 1) * 64], X[:, h, :], ident[0:64, 0:64], [Xk, "cst"], [PK[B0]])
                Y, Yk = pl["Y"].next()
                cp(f4(Y), PS[B0][0:64, 256:512], [PK[B0]], [Yk], eng="act")
                yield
                Qf, Qfk = pl["Qf"].next()
                tt(f4(Qf), f4(gI4), f4(Y), ALU.subtract, ["gI4", Yk], [Qfk])
                for lev in range(5):
                    Xn, Xnk = pl["X"].next()
                    for h in range(4):
                        mm(PS[B1][0:64, h * 64:(h + 1) * 64], Y[:, h, :], X[:, h, :], True, True, [Yk, Xk], [PK[B1]])
                    if lev < 4:
                        Yn, Ynk = pl["Y"].next()
                        for h in range(4):
                            mm(PS[B1][0:64, 256 + h * 64:256 + (h + 1) * 64], X[:, h, :], Y[:, h, :], True, True, [Yk, Xk], [PK[B1]])
                    yield
                    cp(f4(Xn), PS[B1][0:64, 0:256], [PK[B1]], [Xnk], eng="act")
                    if lev < 4:
                        cp(f4(Yn), PS[B1][0:64, 256:512], [PK[B1]], [Ynk], eng="act")
                    yield
                    for h in range(4):
                        mm(PS[B0][0:64, h * 64:(h + 1) * 64], Xn[:, h, :], Qf[:, h, :], True, True, [Xnk, Qfk], [PK[B0]])
                    yield
                    Qfn, Qfnk = pl["Qf"].next()
                    tt(f4(Qfn), f4(Qf), PS[B0][0:64, 0:256], ALU.add, [Qfk, PK[B0]], [Qfnk])
                    X, Xk = Xn, Xnk
                    if lev < 4:
                        Y, Yk = Yn, Ynk
                    Qf, Qfk = Qfn, Qfnk
                    yield
                Qb, Qbk = pl["Qb"].next()
                cp(f4(Qb), f4(Qf), [Qfk], [Qbk], eng="act")
                yield
                for h in range(4):
                    mm(PS[B1][0:64, h * 128:(h + 1) * 128], Qb[:, h, :], vb[:, h, :], True, True, [Qbk, vbk], [PK[B1]])
                    mm(PS[B0][:, 256 + h * 64:256 + (h + 1) * 64], kbe[:, h, :], Qb[:, h, :], True, True, [Qbk, kbek], [PK[B0]])
                yield
                u, uk = pl["u"].next()
                cp(f4(u), PS[B1][0:64, 0:512], [PK[B1]], [uk], eng="act")
                wT, wTk = pl["wT"].next()
                cp(f4(wT), PS[B0][:, 256:512], [PK[B0]], [wTk])
                handoff[d][c] = (kd, kdk, qkm, qkmk, u, uk, wT, wTk)
                prep_n[d] = n_ + 1
                yield

        def rec(d, order):
            pl = pools[d]
            R0, R1 = 4 * d + 2, 4 * d + 3
            Sd, Sbd = S[:, d], Sb[:, d]
            for n_, c in enumerate(order):
                while prep_n[d] < n_ + 1:
                    yield
                kd, kdk, qkm, qkmk, u, uk, wT, wTk = handoff[d].pop(c)
                csl = slice(c * 64, (c + 1) * 64)
                for h in range(4):
                    mm(PS[R0][0:64, h * 128:(h + 1) * 128], wT[:, h, :], Sbd[:, h, :], True, True, [wTk, ("gSb", d)], [PK[R0]])
                    mm(PS[R1][0:64, h * 128:(h + 1) * 128], qT[:, h, csl], Sbd[:, h, :], True, True, ["gqT", ("gSb", d)], [PK[R1]])
                yield
                vn, vnk = pl["vn"].next()
                tt(f4(vn), f4(u), PS[R0][0:64, 0:512], ALU.subtract, [uk, PK[R0]], [vnk])
                tA, tAk = pl["tA"].next()
                tt(tA, v4(PS[R1][0:64, 0:512]), bcn(egc[:, c, :], d, 128), ALU.mult, [PK[R1], "g_egc"], [tAk])
                yield
                for h in range(4):
                    mm(PS[R0][:, h * 128:(h + 1) * 128], kd[:, h, :], vn[:, h, :], True, True, [kdk, vnk], [PK[R0]])
                    mm(PS[R1][0:64, h * 128:(h + 1) * 128], qkm[:, h, :], vn[:, h, :], True, True, [qkmk, vnk], [PK[R1]])
                yield
                tt(Sd, Sd, bass.AP(etot.tensor, etot[:, c, :].offset + d * 4, [list(etot[:, c, :].ap[0]), [1, 4], [0, 128]]), ALU.mult,
                   [("gS", d), "g_etot"], [("gS", d)])
                tt(f4(Sd), f4(Sd), PS[R0][:, 0:512], ALU.add, [("gS", d), PK[R0]], [("gS", d)])
                yield
                cp(f4(Sbd), f4(Sd), [("gS", d)], [("gSb", d)], eng="act")
                tt(f4(tA), f4(tA), PS[R1][0:64, 0:512], ALU.add, [tAk, PK[R1]], [tAk])
                dma(s_go[d, c], f4(tA), [tAk], [("scr_go", d, c)])
                rec_n[d] = n_ + 1
                yield

        orders = [list(range(NCH)), [3, 2, 1, 0] + list(range(NCH - 1, 3, -1))]
        gens = [prep(0, orders[0]), prep(1, orders[1]), rec(0, orders[0]), rec(1, orders[1])]
        while gens:
            for g_ in list(gens):
                try:
                    next(g_)
                except StopIteration:
                    gens.remove(g_)
        P.pop()

        P.push()
        of_r = SRot(P, [64, 2, 512], F32, 2, "gof")
        sq_r = SRot(P, [64, 4, 128], F32, 2, "gsq")
        ss_r = SRot(P, [64, 4], F32, 2, "gss")
        on_r = SRot(P, [64, 4, 128], BF16, 2, "gon")
        gg_r = SRot(P, [128, 4, 64], BF16, 2, "ggc")
        yo_r = SRot(P, [128, 4, 64], BF16, 2, "gyo")
        gg_keys = [("scr", str(s_gg[h])) for h in range(4)]
        gg_pht = s_gg.rearrange("h p t -> p h t")
        for c in range(NCH):
            csl = slice(c * 64, (c + 1) * 64)
            of, ofk = of_r.next()
            for d_ in range(2):
                dma(of[:, d_, :], s_go[d_, c], [("scr_go", d_, c)], [ofk])
            o = of[:, 0, :]
            tt(o, o, of[:, 1, :], ALU.add, [ofk], [ofk], eng="dve")
            sq, sqk = sq_r.next()
            tt(f4(sq), o, o, ALU.mult, [ofk], [sqk], eng="dve")
            ss, ssk = ss_r.next()
            P.emit("dve", lambda e_, o_=ss, i_=sq: e_.tensor_reduce(out=o_, in_=i_, axis=AX.X, op=ALU.add), [sqk], [ssk])
            ts(ss, ss, 1.0 / 128, EPS, ALU.mult, ALU.add, [ssk], [ssk])
            act(ss, ss, AF.Sqrt, [ssk], [ssk])
            P.emit("dve", lambda e_, o_=ss: e_.reciprocal(o_, o_), [ssk], [ssk])
            on, onk = on_r.next()
            tt(on, v4(o), bass.AP(ss.tensor, ss.offset, [list(ss.ap[0]), [1, 4], [0, 128]]), ALU.mult, [ofk, ssk], [onk])
            psb, pbk = P.pbank(bf=True)
            for h in range(4):
                P.tr(psb[:, h * 64:(h + 1) * 64], on[:, h, :], identb[0:64, 0:64], [onk, "cstb"], [pbk])
            gg, ggk = gg_r.next()
            dma(gg, gg_pht[:, :, csl], gg_keys, [ggk])
            yo, yok = yo_r.next()
            stt(f4(yo), psb[:, 0:256], gdnw[:, l:l + 1], f4(gg), ALU.mult, ALU.mult, [pbk, "gdnw", ggk], [yok])
            for h in range(4):
                dma(s_y[8 + h][:, csl], yo[:, h, :], [yok], [("scr_y", 8 + h, c)], q="pool")
        P.pop()

    MIXER_IMPL["gdn"] = gdn_impl

    def ssd_impl(l):
        Uc, Loc = cst[:, 2, :], cst[:, 3, :]
        PS = P.psum
        PK = [f"psum{i}" for i in range(8)]
        NTI = 18
        P.push()
        ytok = P.sb([128, NTI, 512], F32, "ytok")
        sx_keys = [("scr", str(s_sx[j])) for j in range(4)]
        sx_pjt = s_sx.rearrange("j p t -> p j t")
        P.push()
        BT = P.sb([128, 2, T], BF16, "BT")
        CT = P.sb([128, 2, T], BF16, "CT")
        Btok = P.sb([128, NTI, 256], BF16, "Btok")
        for g in range(2):
            dma(BT[:, g, :], s_sbc[g], [("scr", str(s_sbc[g]))], ["BT"])
            dma(CT[:, g, :], s_sbc[2 + g], [("scr", str(s_sbc[2 + g]))], ["CT"])
        dt = P.sb([128, NTI, 16], F32, "dt")
        av = P.sb([128, NTI, 16], F32, "av")
        acum = P.sb([128, NTI, 16], F32, "acum")
        nacum = P.sb([128, NTI, 16], F32, "nacum")
        eacum = P.sb([128, NTI, 16], F32, "eacum")
        atot = P.sb([128, NTI, 16], F32, "atot")
        etot = P.sb([128, NTI, 16], F32, "etot")
        dtw = P.sb([128, NTI, 16], F32, "dtw")
        P.push()
        dtr = P.sb([16, T], F32, "dtr")
        dma(dtr, s_small[0:16, :], [("scr", str(s_small))], ["dtr"])
        for i in range(NTI):
            ps, pk = PS[7], PK[7]
            P.tr(ps[:, 0:16], dtr[0:16, i * 128:(i + 1) * 128], ident[0:16, 0:16], ["dtr", "cst"], [pk])
            tt(dt[:, i, :], ps[:, 0:16], ssdtb[:, l, :], ALU.add, [pk, "ssdtb"], ["dt"])
        P.pop()
        dtf = dt.rearrange("p a b -> p (a b)")
        act(dtf, dtf, AF.Exp, ["dt"], ["dt"])
        act(dtf, dtf, AF.Ln, ["dt"], ["dt"], bias=1.0)
        for i in range(NTI):
            tt(av[:, i, :], dt[:, i, :], ssA[:, l, :], ALU.mult, ["dt", "ssA"], ["av"])
        for i in range(NTI):
            ps, pk = PS[7], PK[7]
            mm(ps[:, 0:8], Uc, av[:, i, 0:8], True, True, ["cst", "av"], [pk])
            mm(ps[:, 8:16], Loc, av[:, i, 8:16], True, True, ["cst", "av"], [pk])
            mm(ps[:, 16:32], ones, av[:, i, :], True, True, ["cst", "av"], [pk])
            cp(acum[:, i, :], ps[:, 0:16], [pk], ["acum"])
            cp(atot[:, i, :], ps[:, 16:32], [pk], ["atot"])
        fl = lambda t_: t_.rearrange("p a b -> p (a b)")
        ts(fl(nacum), fl(acum), -1.0, None, ALU.mult, None, ["acum"], ["nacum"])
        act(fl(eacum), fl(acum), AF.Exp, ["acum"], ["eacum"])
        act(fl(etot), fl(atot), AF.Exp, ["atot"], ["etot"])
        tt(fl(dtw), fl(atot), fl(acum), ALU.subtract, ["atot", "acum"], ["dtw"])
        act(fl(dtw), fl(dtw), AF.Exp, ["dtw"], ["dtw"])
        tt(fl(dtw), fl(dtw), fl(dt), ALU.mult, ["dtw", "dt"], ["dtw"])
        for i in range(NTI):
            psb, pk = P.psum_bf[7], PK[7]
            for g in range(2):
                P.tr(psb[:, g * 128:(g + 1) * 128], BT[:, g, i * 128:(i + 1) * 128], identb, ["BT", "cstb"], [pk])
            cp(Btok[:, i, :], psb[:, 0:256], [pk], ["Btok"], eng="act")

        hin = P.sb([128, 2, 512], F32, "hin")
        hinb = P.sb([128, 2, 512], BF16, "hinb")
        for d_ in range(2):
            P.memset(hin[:, d_, :], 0.0, [("hin", d_)])
            P.memset(hinb[:, d_, :], 0.0, [("hinb", d_)])
        P.memset(ytok.rearrange("p a b -> p (a b)"), 0.0, ["ytok0"])
        for i in range(NTI):
            P.buf[("ytok", i)] = [P.buf["ytok0"][0], {}, []]

        def bc64(ap16, d):
            return bass.AP(ap16.tensor, ap16.offset + d * 8, [list(ap16.ap[0]), [1, 8], [0, 64]])

        v3 = lambda t_: t_.rearrange("p (h o) -> p h o", h=8)

        def sweep(d, order):
            nm = f"s{d}"
            tri = Uc if d == 0 else Loc
            B0, B1, B2, B3 = 4 * d, 4 * d + 1, 4 * d + 2, 4 * d + 3
            aU = P.sb([128, 4, 128], F32, nm + "aU")
            xd_rot = SRot(P, [128, 512], BF16, 2, nm + "xd")
            xdw_rot = SRot(P, [128, 512], BF16, 2, nm + "xdw")
            cbm_rot = SRot(P, [128, 2, 128], F32, 2, nm + "cbm")
            dm_rot = SRot(P, [128, 128], F32, 2, nm + "dmin")
            e_rot = SRot(P, [128, 128], F32, 2, nm + "edec")
            mt_rot = SRot(P, [128, 128], BF16, 2, nm + "mt")
            t5_rot = SRot(P, [128, 512], F32, 2, nm + "t5")
            xs_rot = SRot(P, [128, 4, 128], F32, 2, nm + "xs_t")
            yield
            for i in order:
                tsl = slice(i * 128, (i + 1) * 128)
                xs_, xsk = xs_rot.next()
                dma(xs_, sx_pjt[:, :, tsl], sx_keys, [xsk])
                for j in range(4):
                    P.tr(PS[B0][:, j * 128:(j + 1) * 128], xs_[:, j, :], ident, [xsk, "cst"], [PK[B0]])
                for g in range(2):
                    mm(PS[B3][:, g * 256:(g + 1) * 256], CT[:, g, tsl], hinb[:, d, g * 256:(g + 1) * 256], True, True,
                       ["CT", ("hinb", d)], [PK[B3]])
                yield
                xd, xdk = xd_rot.next()
                xdw, xdwk = xdw_rot.next()
                tt(v3(xd), v3(PS[B0][:, 0:512]), bc64(dt[:, i, :], d), ALU.mult, [PK[B0], "dt"], [xdk])
                tt(v3(xdw), v3(PS[B0][:, 0:512]), bc64(dtw[:, i, :], d), ALU.mult, [PK[B0], "dtw"], [xdwk])
                yield
                cbm, cbk = cbm_rot.next()
                for g in range(2):
                    mm(PS[B0][:, g * 128:(g + 1) * 128], BT[:, g, tsl], CT[:, g, tsl], True, True, ["BT", "CT"], [PK[B0]])
                yield
                for g in range(2):
                    tt(cbm[:, g, :], PS[B0][:, g * 128:(g + 1) * 128], tri, ALU.mult, [PK[B0], "cst"], [cbk])
                t5, t5k = t5_rot.next()
                tt(v3(t5), v3(PS[B3][:, 0:512]), bc64(eacum[:, i, :], d), ALU.mult, [PK[B3], "eacum"], [t5k])
                yield
                for g in range(2):
                    mm(PS[B0][:, g * 256:(g + 1) * 256], Btok[:, i, g * 128:(g + 1) * 128], xdw[:, g * 256:(g + 1) * 256], True, True,
                       ["Btok", xdwk], [PK[B0]])
                for rnd in range(2):
                    for hq in range(4):
                        h = rnd * 4 + hq
                        act(aU[:, hq, :], tri, AF.Copy, ["cst", "av"], [(nm + "aU", hq)], scale=av[:, i, d * 8 + h:d * 8 + h + 1])
                    yield
                    mm(PS[B1][:, 0:512], ones, aU.rearrange("p a b -> p (a b)"), True, True,
                       ["cst"] + [(nm + "aU", q_) for q_ in range(4)], [PK[B1]])
                    yield
                    for hq in range(4):
                        h = rnd * 4 + hq
                        dmin, dmk = dm_rot.next()
                        ts(dmin, PS[B1][:, hq * 128:(hq + 1) * 128], nacum[:, i, d * 8 + h:d * 8 + h + 1], 0.0,
                           ALU.add, ALU.min, [PK[B1], "nacum"], [dmk])
                        yield
                        ed, edk = e_rot.next()
                        act(ed, dmin, AF.Exp, [dmk], [edk])
                        yield
                        mt, mtk = mt_rot.next()
                        tt(mt, cbm[:, h // 4, :], ed, ALU.mult, [cbk, edk], [mtk])
                        yield
                        mm(PS[B2][:, h * 64:(h + 1) * 64], mt, xd[:, h * 64:(h + 1) * 64], True, True, [mtk, xdk], [PK[B2]])
                yield
                tt(t5, PS[B2][:, 0:512], t5, ALU.add, [PK[B2], t5k], [t5k])
                tt(ytok[:, i, :], ytok[:, i, :], t5, ALU.add, [("ytok", i), t5k], [("ytok", i)], eng="dve")
                tt(v3(hin[:, d, :]), v3(hin[:, d, :]), bc64(etot[:, i, :], d), ALU.mult, [("hin", d), "etot"], [("hin", d)])
                yield
                tt(hin[:, d, :], hin[:, d, :], PS[B0][:, 0:512], ALU.add, [("hin", d), PK[B0]], [("hin", d)])
                yield
                cp(hinb[:, d, :], hin[:, d, :], [("hin", d)], [("hinb", d)], eng="act")
                yield

        gens = [sweep(0, list(range(NTI))), sweep(1, [1, 0] + list(range(NTI - 1, 1, -1)))]
        while gens:
            for g_ in list(gens):
                try:
                    next(g_)
                except StopIteration:
                    gens.remove(g_)
        P.pop()
        szr = SRot(P, [128, 4, 512], BF16, 2, "szt")
        y2r = SRot(P, [128, 4, 512], F32, 2, "y2")
        sqr = SRot(P, [128, 512], F32, 2, "ysq")
        rsr = SRot(P, [128, 2, 512], F32, 2, "yrs")
        obr = SRot(P, [128, 4, 512], BF16, 2, "yob")
        xsr = SRot(P, [128, 4, 512], F32, 2, "xsf")
        for (t0, n, w) in TT:
            sz, szk = szr.next()
            for j in range(4):
                dma(sz[:, j, 0:n], s_sz[j][:, t0:t0 + n], [("scr", str(s_sz[j]))], [(szk, j)])
            y2, y2k = y2r.next()
            xsT, xsTk = xsr.next()
            dma(xsT[:, :, 0:n], sx_pjt[:, :, t0:t0 + n], sx_keys, [xsTk])
            for j in range(4):
                for q_ in range(n // 128):
                    i = t0 // 128 + q_
                    P.tr(PS[j][:, q_ * 128:(q_ + 1) * 128], ytok[:, i, j * 128:(j + 1) * 128], ident, [("ytok", i), "cst"], [PK[j]])
                stt(y2[:, j, 0:n], xsT[:, j, 0:n], ssDpc[:, l, j:j + 1], PS[j][:, 0:n], ALU.mult, ALU.add,
                    [xsTk, "ssDpc", PK[j]], [y2k])
                tt(y2[:, j, 0:n], y2[:, j, 0:n], sz[:, j, 0:n], ALU.mult, [y2k, (szk, j)], [y2k])
            rs, rsk = rsr.next()
            for g in range(2):
                for jj in range(2):
                    sq, sqk = sqr.next()
                    act(sq[:, 0:n], y2[:, 2 * g + jj, 0:n], AF.Square, [y2k], [sqk])
                    mm(PS[4 + g][:, 0:n], ones, sq[:, 0:n], jj == 0, jj == 1, ["cst", sqk], [PK[4 + g]])
                ts(rs[:, g, 0:n], PS[4 + g][:, 0:n], 1.0 / 256, EPS, ALU.mult, ALU.add, [PK[4 + g]], [rsk])
                act(rs[:, g, 0:n], rs[:, g, 0:n], AF.Sqrt, [rsk], [rsk])
                P.emit("dve", lambda e, o=rs[:, g, 0:n]: e.reciprocal(o, o), [rsk], [rsk])
            ob, obk = obr.next()
            for j in range(4):
                stt(ob[:, j, 0:n], y2[:, j, 0:n], ssnw[:, l, j:j + 1], rs[:, j // 2, 0:n], ALU.mult, ALU.mult, [y2k, "ssnw", rsk], [obk])
                dma(s_y[4 + j][:, t0:t0 + n], ob[:, j, 0:n], [obk], [("scr_y", 4 + j, t0)])
        P.pop()

    MIXER_IMPL["ssm"] = ssd_impl
    hyena = MIXER_IMPL.get("hy", hyena)
    ssd = MIXER_IMPL.get("ssm", ssd)
    gdn = MIXER_IMPL.get("gdn", gdn)

    for l in range(nlayers):
        layer(l)

    P.push()
    oT = P.sb([128, 8, NLAT], F32, "oT")
    xl = None
    norm_mod_out = oT

    class _Shift:
        pass
    lat_tiles = [(t0, n, 0) for (t0, n, w) in TT if w == 0]
    sq_rot = SRot(P, [128, 512], BF16, 3, "fsq")
    rstd_rot = SRot(P, [128, 512], F32, 2, "frstd")
    for (t0, n, w) in lat_tiles:
        ps, pk = P.pbank()
        for c in range(8):
            sq, sqk = sq_rot.next()
            act(sq[:, :n], xT[:, c, t0:t0 + n], AF.Square, [("xT", c)], [sqk])
            mm(ps[:, :n], onesb, sq[:, :n], c == 0, c == 7, ["cstb", sqk], [pk])
        rstd, rk = rstd_rot.next()
        ts(rstd[:, :n], ps[:, :n], 1.0 / D, EPS, ALU.mult, ALU.add, [pk], [rk])
        act(rstd[:, :n], rstd[:, :n], AF.Sqrt, [rk], [rk])
        P.emit("dve", lambda e, o=rstd[:, :n]: e.reciprocal(o, o), [rk], [rk])
        for c in range(8):
            stt(oT[:, c, t0 - NCTX:t0 - NCTX + n], xT[:, c, t0:t0 + n], fnw[:, c:c + 1], rstd[:, :n], ALU.mult, ALU.mult,
                [("xT", c), rk, "fnw"], [("oT", c, t0)])
            dma(d_outT[:, c, t0 - NCTX:t0 - NCTX + n], oT[:, c, t0 - NCTX:t0 - NCTX + n], [("oT", c, t0)], [("OUT", c, t0)], is_out=True)
    P.pop()
    P.finish()
    P.build()
    return nc, list(dbg_out.keys())


def kernel(**inputs):
    nc, _ = build_program()
    sh = prep_shared(inputs)
    in_maps = [dict(sh, **prep_core(inputs, b)) for b in range(8)]
    res = run_bass_kernel_spmd(nc, in_maps, core_ids=list(range(8)))
    out = np.stack([np.asarray(r["outT"], np.float32).transpose(2, 1, 0).reshape(NLAT, D) for r in res.results])
    return np.ascontiguousarray(out.astype(np.float32))
```
